# Optimizing a Trainium2 kernel written in Bass

```python
import jax, jax.numpy as jnp
from jax import lax
import numpy as np

D_MODEL = 2048
BATCH = 4
SEQ = 2048
DEPTH = 1

GLA_VALUE_WIDTH = D_MODEL // 2
GLA_HEADS = 4
GLA_DV = GLA_VALUE_WIDTH // GLA_HEADS
GLA_DK = GLA_DV // 2
GLA_KEY_WIDTH = GLA_HEADS * GLA_DK
GLA_GATE_RANK = 16
GLA_GATE_NORMALIZER = 16.0
HGRN_WIDTH = D_MODEL // 2
HGRN_EXPAND = 128
HGRN_HEADS = HGRN_WIDTH // HGRN_EXPAND
HGRN_DV = HGRN_WIDTH // HGRN_HEADS
CHUNK = 64
D_FF = 5504
EPS = 1e-6

SPLIT_SIZES = (
    GLA_KEY_WIDTH,
    GLA_KEY_WIDTH,
    GLA_VALUE_WIDTH,
    GLA_VALUE_WIDTH,
    GLA_GATE_RANK,
    HGRN_WIDTH,
    HGRN_WIDTH,
    HGRN_WIDTH,
    HGRN_WIDTH,
    D_MODEL,
    D_MODEL,
)
IN_WIDTH = 2 * GLA_KEY_WIDTH + 2 * GLA_VALUE_WIDTH + GLA_GATE_RANK + 4 * HGRN_WIDTH + 2 * D_MODEL

kernel_name = "hybrid_gla_hgrn2_macaron_sandwich"


def rms_norm(x, w):
    xf = x.astype(jnp.float32)
    y = xf * lax.rsqrt(jnp.mean(xf * xf, axis=-1, keepdims=True) + EPS)
    return (y * w.astype(jnp.float32)).astype(x.dtype)


def swiglu(x, w_gate, w_up, w_down):
    return (jax.nn.silu(x @ w_gate) * (x @ w_up)) @ w_down


def split_heads(t, n_heads):
    b, l, w = t.shape
    return t.reshape(b, l, n_heads, w // n_heads).transpose(0, 2, 1, 3)


def merge_heads(t):
    b, h, l, d = t.shape
    return t.transpose(0, 2, 1, 3).reshape(b, l, h * d)


def chunked_gated_linear_attention(q, k, v, log_a):
    q, k, v, log_a = (t.astype(jnp.float32) for t in (q, k, v, log_a))
    b, h, l, dk = q.shape
    dv = v.shape[-1]
    n = l // CHUNK

    def to_chunks(t):
        return t.reshape(b, h, n, CHUNK, t.shape[-1]).transpose(2, 0, 1, 3, 4)

    causal = jnp.tril(jnp.ones((CHUNK, CHUNK), dtype=bool))[:, :, None]

    def step(state, inp):
        qi, ki, vi, gi = inp
        cum = jnp.cumsum(gi, axis=-2)
        cum_last = cum[..., -1:, :]
        diff = cum[..., :, None, :] - cum[..., None, :, :]
        decay = jnp.exp(jnp.where(causal, diff, -jnp.inf))
        scores = jnp.einsum('bhid,bhjd,bhijd->bhij', qi, ki, decay)
        out = (jnp.einsum('bhij,bhjv->bhiv', scores, vi)
               + jnp.einsum('bhid,bhdv->bhiv', qi * jnp.exp(cum), state))
        new_state = (jnp.exp(cum_last)[..., 0, :, None] * state
                     + jnp.einsum('bhjd,bhjv->bhdv', ki * jnp.exp(cum_last - cum), vi))
        return new_state, out

    state0 = jnp.zeros((b, h, dk, dv), jnp.float32)
    _, out = lax.scan(step, state0, (to_chunks(q), to_chunks(k), to_chunks(v), to_chunks(log_a)))
    return out.transpose(1, 2, 0, 3, 4).reshape(b, h, l, dv)


def hybrid_token_mixer(u, layer, w_in, gla_w_gk_up, gla_b_gk, gla_norm, hgrn_lb_logits,
                       hgrn_norm, w_branch_gla, w_branch_hgrn, b_branch_gates, w_out):
    proj = u @ w_in
    offsets = np.cumsum(SPLIT_SIZES)[:-1].tolist()
    (g_q, g_k, g_v, g_out, g_code, h_q, h_f, h_i, h_out, z_gla, z_hgrn) = jnp.split(proj, offsets, axis=-1)

    gla_log_a = jax.nn.log_sigmoid(g_code @ gla_w_gk_up + gla_b_gk) / GLA_GATE_NORMALIZER
    o_gla = chunked_gated_linear_attention(
        split_heads(g_q * (GLA_DK ** -0.5), GLA_HEADS), split_heads(g_k, GLA_HEADS),
        split_heads(g_v, GLA_HEADS), split_heads(gla_log_a, GLA_HEADS))
    o_gla = merge_heads(rms_norm(o_gla, gla_norm)).astype(u.dtype) * jax.nn.silu(g_out)

    lb = jnp.cumsum(jax.nn.softmax(hgrn_lb_logits.astype(jnp.float32), axis=0), axis=0)[layer]
    log_f = jnp.logaddexp(jnp.log(lb), jnp.log1p(-lb) + jax.nn.log_sigmoid(h_f.astype(jnp.float32)))
    h_k = -jnp.expm1(log_f)
    o_hgrn = chunked_gated_linear_attention(
        split_heads(jax.nn.silu(h_q), HGRN_HEADS), split_heads(h_k, HGRN_HEADS),
        split_heads(h_i, HGRN_HEADS), split_heads(log_f, HGRN_HEADS))
    o_hgrn = merge_heads(rms_norm(o_hgrn, hgrn_norm)).astype(u.dtype) * jax.nn.silu(h_out)

    merged = (jax.nn.sigmoid(z_gla + b_branch_gates[0]) * (o_gla @ w_branch_gla)
              + jax.nn.sigmoid(z_hgrn + b_branch_gates[1]) * (o_hgrn @ w_branch_hgrn))
    return merged @ w_out


def setup_inputs(seed: int = 0) -> dict:
    key = jax.random.key(seed)
    ks = jax.random.split(key, 24)

    def dense(k, shape, fan_in):
        return jax.random.normal(k, shape, jnp.float32) * (fan_in ** -0.5)

    def gain(k, shape):
        return 1.0 + 0.05 * jax.random.normal(k, shape, jnp.float32)

    return {
        "x": jax.random.normal(ks[0], (BATCH, SEQ, D_MODEL), jnp.float32),
        "ffn1_pre_norm": gain(ks[1], (DEPTH, D_MODEL)),
        "ffn1_w_gate": dense(ks[2], (DEPTH, D_MODEL, D_FF), D_MODEL),
        "ffn1_w_up": dense(ks[3], (DEPTH, D_MODEL, D_FF), D_MODEL),
        "ffn1_w_down": dense(ks[4], (DEPTH, D_FF, D_MODEL), D_FF),
        "ffn1_post_norm": gain(ks[5], (DEPTH, D_MODEL)),
        "mix_pre_norm": gain(ks[6], (DEPTH, D_MODEL)),
        "w_in": dense(ks[7], (DEPTH, D_MODEL, IN_WIDTH), D_MODEL),
        "gla_w_gk_up": dense(ks[8], (DEPTH, GLA_GATE_RANK, GLA_KEY_WIDTH), GLA_GATE_RANK),
        "gla_b_gk": 0.02 * jax.random.normal(ks[9], (DEPTH, GLA_KEY_WIDTH), jnp.float32),
        "gla_norm": gain(ks[10], (DEPTH, GLA_DV)),
        "hgrn_lb_logits": 0.1 * jax.random.normal(ks[11], (DEPTH + 1, HGRN_WIDTH), jnp.float32),
        "hgrn_norm": gain(ks[12], (DEPTH, HGRN_DV)),
        "w_branch_gla": dense(ks[13], (DEPTH, GLA_VALUE_WIDTH, D_MODEL), GLA_VALUE_WIDTH),
        "w_branch_hgrn": dense(ks[14], (DEPTH, HGRN_WIDTH, D_MODEL), HGRN_WIDTH),
        "b_branch_gates": 0.02 * jax.random.normal(ks[15], (DEPTH, 2, D_MODEL), jnp.float32),
        "w_out": dense(ks[16], (DEPTH, D_MODEL, D_MODEL), D_MODEL),
        "mix_post_norm": gain(ks[17], (DEPTH, D_MODEL)),
        "ffn2_pre_norm": gain(ks[18], (DEPTH, D_MODEL)),
        "ffn2_w_gate": dense(ks[19], (DEPTH, D_MODEL, D_FF), D_MODEL),
        "ffn2_w_up": dense(ks[20], (DEPTH, D_MODEL, D_FF), D_MODEL),
        "ffn2_w_down": dense(ks[21], (DEPTH, D_FF, D_MODEL), D_FF),
        "ffn2_post_norm": gain(ks[22], (DEPTH, D_MODEL)),
    }


def reference(x, ffn1_pre_norm, ffn1_w_gate, ffn1_w_up, ffn1_w_down, ffn1_post_norm,
              mix_pre_norm, w_in, gla_w_gk_up, gla_b_gk, gla_norm, hgrn_lb_logits, hgrn_norm,
              w_branch_gla, w_branch_hgrn, b_branch_gates, w_out, mix_post_norm,
              ffn2_pre_norm, ffn2_w_gate, ffn2_w_up, ffn2_w_down, ffn2_post_norm):
    h = x
    for l in range(DEPTH):
        f1 = swiglu(rms_norm(h, ffn1_pre_norm[l]), ffn1_w_gate[l], ffn1_w_up[l], ffn1_w_down[l])
        h = h + 0.5 * rms_norm(f1, ffn1_post_norm[l])
        m = hybrid_token_mixer(rms_norm(h, mix_pre_norm[l]), l, w_in[l], gla_w_gk_up[l], gla_b_gk[l],
                               gla_norm[l], hgrn_lb_logits, hgrn_norm[l], w_branch_gla[l],
                               w_branch_hgrn[l], b_branch_gates[l], w_out[l])
        h = h + rms_norm(m, mix_post_norm[l])
        f2 = swiglu(rms_norm(h, ffn2_pre_norm[l]), ffn2_w_gate[l], ffn2_w_up[l], ffn2_w_down[l])
        h = h + 0.5 * rms_norm(f2, ffn2_post_norm[l])
    return h
```

```python
import numpy as np
import concourse.bass as bass
import concourse.mybir as mybir
from concourse.bass_utils import run_bass_kernel_spmd

F32 = mybir.dt.float32
BF16 = mybir.dt.bfloat16
U8 = mybir.dt.uint8
AF = mybir.ActivationFunctionType
ALU = mybir.AluOpType

D = 2048
T = 1024
NKB = 16
DFF = 5504
NFB = 43
INW = 11280
EPS = 1e-6
NB = 3
C_GQ, C_GK, C_GV, C_GO, C_CODE = 0, 512, 1024, 2048, 3072
C_HQ, C_HF, C_HI, C_HO, C_ZG, C_ZH = 3088, 4112, 5136, 6160, 7184, 9232

P_F1PRE, P_F1POST, P_MPRE, P_MPOST, P_F2PRE, P_F2POST = 0, 16, 32, 48, 64, 80
P_BGK, P_GNORM, P_HNORM, P_L0, P_L1, P_BB0, P_BB1, P_FLAG = 96, 100, 102, 103, 111, 119, 135, 151
NPAR = 152


class Tok:
    __slots__ = ("eng", "idx", "used", "ms")

    def __init__(self, eng, idx):
        self.eng, self.idx, self.used, self.ms = eng, idx, False, None


class Prog:
    ENGS = ("pe", "act", "dve", "pool", "sp")

    def __init__(self):
        self.q = {e: [] for e in self.ENGS}
        self.dcnt = {}
        self.last = {}

    def wait(self, eng, tok):
        if tok is None:
            return
        if isinstance(tok, (list, tuple)) and not (len(tok) == 2 and isinstance(tok[0], str)):
            for t in tok:
                self.wait(eng, t)
            return
        if isinstance(tok, Tok):
            if tok.eng == eng:
                return
            tok.used = True
        self.q[eng].append(("wait", tok))

    def op(self, eng, fn, deps=()):
        self.wait(eng, deps)
        if eng in ("act", "dve"):
            prev = self.last.get(eng)
            if prev is not None:
                prev.used = True
                self.q[eng].append(("wait", prev))
        tok = Tok(eng, len(self.q[eng]))
        self.q[eng].append(("op", fn, tok))
        self.last[eng] = tok
        return tok

    def dma(self, eng, fn, semkey, deps=(), inc=16):
        self.wait(eng, deps)
        self.dcnt[semkey] = self.dcnt.get(semkey, 0) + inc
        tok = (semkey, self.dcnt[semkey])
        self.q[eng].append(("dma", fn, semkey))
        return tok


def build_program():
    nc = bass.Bass("TRN2", target_bir_lowering=False)
    STOP = _STOP[0]
    declared = []

    def dt_in(name, shape, need=0):
        if STOP < need:
            return None
        declared.append(name)
        return nc.dram_tensor(name, shape, F32, kind="ExternalInput").ap()
    x = dt_in("x", [T, D])
    params = dt_in("params", [128, NPAR])
    cident = dt_in("c_ident", [128, 128])
    cmask = dt_in("c_mask", [64, 64])
    w1g = dt_in("ffn1_w_gate", [D, DFF], 1); w1u = dt_in("ffn1_w_up", [D, DFF], 1); w1d = dt_in("ffn1_w_down", [DFF, D], 1)
    w2g = dt_in("ffn2_w_gate", [D, DFF], 3); w2u = dt_in("ffn2_w_up", [D, DFF], 3); w2d = dt_in("ffn2_w_down", [DFF, D], 3)
    w_in = dt_in("w_in", [D, INW], 2)
    wgk = dt_in("gla_w_gk_up", [16, 512], 2)
    wbg = dt_in("w_branch_gla", [1024, D], 2); wbh = dt_in("w_branch_hgrn", [1024, D], 2)
    w_out = dt_in("w_out", [D, D], 2)
    out = nc.dram_tensor("out", [T, D], F32, kind="ExternalOutput").ap()
    cc_src = [nc.dram_tensor(f"cc_src{h}", [128, 256 if h < 4 else 128], F32) for h in range(12)]
    cc_dst = [nc.dram_tensor(f"cc_dst{h}", [256, 256 if h < 4 else 128], F32) for h in range(12)]

    P = Prog()
    ARENA = 212800
    arena_h = nc.alloc_sbuf_tensor("arena", [128, ARENA], U8)
    arena = arena_h.ap() if hasattr(arena_h, "ap") else arena_h
    ps_h = nc.alloc_psum_tensor("ps", [128, 8, 512], F32)
    ps = ps_h.ap() if hasattr(ps_h, "ap") else ps_h

    off = [0]

    def carve(nbytes, dtype, parts=128, at=None):
        if at is None:
            a = off[0]
            off[0] += (nbytes + 31) // 32 * 32
            assert off[0] <= ARENA, off[0]
        else:
            a = at
        return arena[0:parts, a:a + nbytes].bitcast(dtype)

    hT = carve(65536, F32).rearrange("p (i t) -> p i t", i=NKB)
    uT_off = off[0]
    uT = carve(32768, BF16).rearrange("p (i t) -> p i t", i=NKB)
    fT = uT
    big_off = off[0]
    HT = carve(88064, BF16).rearrange("p (i t) -> p i t", i=NFB)
    wring = carve(NB * 4096, BF16).rearrange("p (b k c) -> p b k c", b=NB, k=16)
    rstd = carve(4096, F32)
    sqb = carve(4096, BF16).rearrange("p (b t) -> p b t", b=2)
    tmp_off = off[0]
    tmpA = carve(2048, F32)
    tmpB = carve(2048, F32)
    par = carve(NPAR * 4, F32)
    ones_b = carve(256, BF16)
    tl = big_off + 80896
    ident_f = carve(512, F32, at=tl)
    maskf = carve(256, F32, parts=64, at=tl + 512)
    wgk_f = carve(2048, F32, parts=16, at=tl + 768)
    wgk_b = carve(1024, BF16, parts=16, at=tl + 2816)
    ident_b = carve(256, BF16, at=tl + 3840)
    lbt = carve(8 * 4 * 4, F32).rearrange("p (a j) -> p a j", a=4)
    nbgk = carve(16, F32)
    one1 = carve(4, F32)
    onec = one1[:, 0:1].to_broadcast([128, 1024])
    xstage = [carve(8192, F32, at=big_off + i * 8192) for i in range(8)]
    oT_all = carve(32768, BF16, at=big_off).rearrange("p (i t) -> p i t", i=NKB)
    codeT = carve(2048, BF16, parts=16, at=big_off + 32768)
    tb = big_off + 34816
    mergedT = carve(32768, BF16, at=tb).rearrange("p (i t) -> p i t", i=NKB)
    Qt = carve(2048, BF16, at=tb + 0)
    Qb = carve(2048, BF16, at=tb + 2048)
    Kt = carve(2048, BF16, at=tb + 4096)
    Kh = carve(2048, BF16, at=tb + 6144)
    vT = carve(4096, BF16, at=tb + 8192).rearrange("p (b t) -> p b t", b=2)
    pb = tb + 12288
    Gt = carve(4096, F32, at=pb)
    EA = carve(4096, F32, at=pb + 4096)
    EB = carve(4096, F32, at=pb + 8192)
    EC = carve(4096, F32, at=pb + 12288)
    kTf = carve(4096, F32, at=pb + 16384)
    qf = carve(4096, F32, at=pb + 20480)
    A_sb = carve(2048, BF16, parts=64, at=pb + 4096)
    Ktok = carve(4096, BF16, parts=64, at=pb + 6144).rearrange("p (c d) -> p c d", c=16)
    vtok = carve(4096, BF16, parts=64, at=pb + 10240).rearrange("p (c d) -> p c d", c=16)
    Sb = carve(4096, BF16, at=pb + 14336).rearrange("p (c d) -> p c d", c=16)
    sg = carve(4096, F32, at=tmp_off)
    Sst2 = carve(2048, F32, at=tl + 4096).rearrange("p (b q d) -> p b q d", b=2, q=2)
    Sin = carve(1024, F32, at=tl + 6144)
    sm = pb + 24576
    Sin_b = carve(512, BF16, at=sm)
    Gb = carve(68, F32, at=sm + 512)
    nGb = carve(68, F32, at=sm + 640)
    dG = carve(64, F32, at=sm + 768)
    dec = carve(64, F32, at=sm + 896)
    o_raw = carve(8192, F32, at=sm + 1024).rearrange("p (b t) -> p b t", b=2)
    assert sm + 1024 + 8192 <= big_off + 88064
    ostage = [carve(8192, F32, at=big_off + i * 8192) for i in range(8)]

    PSR = {"X": ps[:, 0:2, :], "Y": ps[:, 2:4, :], "W": ps[:, 4:6, :], "Z": ps[:, 6:8, :]}
    flat = lambda a: a.rearrange("p a b -> p (a b)")
    psfree = {k: None for k in PSR}

    wstate = {"i": 0, "free": [None] * NB}

    def slab(src, k0, nk, c0, ncols):
        i = wstate["i"]; wstate["i"] += 1
        b = i % NB
        dst = wring[:, b, 0:nk, 0:ncols]
        s = src[k0 * 128:(k0 + nk) * 128, c0:c0 + ncols].rearrange("(k p) c -> p k c", p=128)
        tok = P.dma("pool", lambda e, dst=dst, s=s: e.dma_start(out=dst, in_=s), f"wld{b}",
                    deps=[wstate["free"][b]])
        return wring[:, b], tok, b

    def proj(src, k0s, c0, ncols, rhs_fn, region, extra_deps=(), M=128):
        ktot = sum(nk for _, nk in k0s)
        kk = 0
        last = None
        first = True
        for (k0, nk) in k0s:
            wv, ltok, b = slab(src, k0, nk, c0, ncols)
            for k in range(nk):
                for half in range(2):
                    deps = []
                    if first:
                        deps = [ltok, psfree[region]] + list(extra_deps)
                    elif k == 0 and half == 0:
                        deps = [ltok]
                    first = False
                    o = flat(PSR[region])[0:M, half * 512:(half + 1) * 512]
                    l = wv[:, k, 0:ncols]
                    r = rhs_fn(k0 + k, half)
                    st, sp_ = (kk == 0), (kk == ktot - 1)
                    last = P.op("pe", lambda e, o=o, l=l, r=r, st=st, sp_=sp_: e.matmul(o, l, r, start=st, stop=sp_), deps)
                kk += 1
            wstate["free"][b] = last
        return last

    u_rhs = lambda k, half: uT[:, k, half * 512:(half + 1) * 512]

    sp_tok = {}
    t_par = P.dma("sp", lambda e: e.dma_start(out=par, in_=params[:, :]), "m0")
    t_id = P.dma("sp", lambda e: e.dma_start(out=ident_f, in_=cident[:, :]), "m1")
    P.op("dve", lambda e: e.memset(ones_b, 1.0), [t_par])
    wsc_all = {}
    for (wcol_, fac_) in ((P_F1POST, 0.5), (P_MPOST, 1.0), (P_F2POST, 0.5)):
        w_ = carve(64, F32)
        P.op("dve", lambda e, w_=w_, wcol_=wcol_, fac_=fac_: e.tensor_scalar(w_, par[:, wcol_:wcol_ + 16], float(fac_), None, ALU.mult))
        wsc_all[(wcol_, float(fac_))] = w_
    P.op("dve", lambda e: e.memset(one1, 1.0))
    P.op("dve", lambda e: e.tensor_tensor(lbt[:, 0, :], par[:, P_L0:P_L0 + 8], par[:, P_L1:P_L1 + 8], ALU.subtract))
    t_set = P.op("dve", lambda e: e.tensor_scalar(nbgk, par[:, P_BGK:P_BGK + 4], -1.0, None, ALU.mult))
    P.op("act", lambda e: e.activation(lbt[:, 1, :], lbt[:, 0, :], AF.Sigmoid), [t_set])
    t_lb = P.op("act", lambda e: e.activation(lbt[:, 2, :], lbt[:, 0, :], AF.Sigmoid, scale=-1.0))

    xs_free = [None, None]
    grp_regions = ["X", "Y", "W", "Z"]
    xts = [P.dma("sp", lambda e, t=t: e.dma_start(out=xstage[t], in_=x[t * 128:(t + 1) * 128, :]), f"xl{t}") for t in range(8)]
    for t in range(8):
        sbuf = xstage[t]
        xt = xts[t]
        lastpe = None
        for g in range(4):
            reg = grp_regions[g]
            pe_t = None
            for q4 in range(4):
                i = g * 4 + q4
                o = PSR[reg][:, q4 // 4 + 0, :] if False else flat(PSR[reg])[:, q4 * 128:(q4 + 1) * 128]
                inn = sbuf[:, i * 128:(i + 1) * 128]
                deps = [xt, t_id, psfree[reg]] if q4 == 0 else []
                pe_t = P.op("pe", lambda e, o=o, inn=inn: e.transpose(o, inn, ident_f), deps)
            dst = hT[:, g * 4:(g + 1) * 4, t * 128:(t + 1) * 128]
            srcv = flat(PSR[reg])[:, 0:512].rearrange("p (a b) -> p a b", a=4)
            eng = "act" if g % 2 == 0 else "dve"
            if eng == "act":
                psfree[reg] = P.op("act", lambda e, dst=dst, srcv=srcv: e.copy(dst, srcv), [pe_t])
            else:
                psfree[reg] = P.op("dve", lambda e, dst=dst, srcv=srcv: e.tensor_copy(dst, srcv), [pe_t])
            lastpe = pe_t
        xs_free[t % 2] = lastpe

    sq_free = [None, None]

    def sumsq_accumulate(src_fn, nblk, region="W", src_deps=(), split=False):
        last = None
        for i in range(nblk):
            b = i % 2
            deps_i = src_deps[i] if isinstance(src_deps, dict) else src_deps
            if split and b == 1:
                s_t = P.op("dve", lambda e, i=i, b=b: e.tensor_tensor(sqb[:, b, :], src_fn(i), src_fn(i), ALU.mult),
                           [sq_free[b]] + list(deps_i if deps_i else []))
            else:
                s_t = P.op("act", lambda e, i=i, b=b: e.activation(sqb[:, b, :], src_fn(i), AF.Square),
                           [sq_free[b]] + list(deps_i if deps_i else []))
            for half in range(2):
                o = flat(PSR[region])[:, half * 512:(half + 1) * 512]
                r = sqb[:, b, half * 512:(half + 1) * 512]
                deps = [s_t] + ([psfree[region]] if i == 0 and half == 0 else [])
                last = P.op("pe", lambda e, o=o, r=r, st=(i == 0), sp_=(i == nblk - 1): e.matmul(o, ones_b, r, start=st, stop=sp_), deps)
            sq_free[b] = last
        return last

    def rstd_from(region, n, pe_tok):
        a = P.op("act", lambda e: e.activation(rstd, flat(PSR[region]), AF.Ln, scale=1.0 / n, bias=EPS_AP), [pe_tok, t_eps])
        psfree[region] = a
        return P.op("act", lambda e: e.activation(rstd, rstd, AF.Exp, scale=-0.5))

    eps_t = carve(4, F32)
    EPS_AP = eps_t[:, 0:1]
    t_eps = P.op("dve", lambda e: e.memset(eps_t, EPS))

    def prenorm(wcol, deps=()):
        sd = {i: [upd_tok[i]] for i in range(NKB)} if (upd_tok and deps) else list(deps)
        pe_tok = sumsq_accumulate(lambda i: hT[:, i, :], NKB, "W", src_deps=sd, split=not isinstance(sd, dict))
        r_t = rstd_from("W", D, pe_tok)
        last = None
        for i in range(NKB):
            last = P.op("dve", lambda e, i=i: e.scalar_tensor_tensor(uT[:, i, :], hT[:, i, :], par[:, wcol + i:wcol + i + 1], rstd, ALU.mult, ALU.mult),
                        [t_par, r_t])
        return last

    upd_tok = {}
    marks = {}

    def down_proj(W, nkb, rhsT, wcol, factor, extra_deps=(), final=False):
        pieces = []
        k0 = 0
        while k0 < nkb:
            nk = min(16, nkb - k0)
            pieces.append((k0, nk)); k0 += nk
        ss_last = None
        for i in range(NKB):
            reg = "X" if i % 2 == 0 else "Y"
            pe_t = proj(W, pieces, i * 128, 128, lambda k, half: rhsT[:, k, half * 512:(half + 1) * 512], reg,
                        extra_deps=extra_deps if i == 0 else ())
            b = i % 2
            for half in range(2):
                sl = slice(half * 512, (half + 1) * 512)
                if half == 0:
                    c_t = P.op("dve", lambda e, i=i, reg=reg, sl=sl: e.tensor_scalar(fT[:, i, sl], flat(PSR[reg])[:, sl], 1.0, None, ALU.mult), [pe_t])
                else:
                    c_t = P.op("dve", lambda e, i=i, reg=reg, sl=sl: e.tensor_scalar(fT[:, i, sl], flat(PSR[reg])[:, sl], 1.0, None, ALU.mult), [pe_t])
                s_t = c_t if half == 1 else c_t
                if half == 0:
                    c0_t = c_t
            psfree[reg] = [c0_t, c_t]
        marks["dp_done"] = pe_t
        ss_last = sumsq_accumulate(lambda i: fT[:, i, :], NKB, "W", src_deps=[c_t], split=True)
        r_t = rstd_from("W", D, ss_last)
        if _SUB[0] == 3:
            return [r_t, c_t]
        wsc = wsc_all[(wcol, float(factor))]
        last = None
        if final:
            marks["half"] = []
            for half in range(2):
                sl = slice(half * 512, (half + 1) * 512)
                for i in range(NKB):
                    tt = tmpA if i % 2 == 0 else tmpB
                    P.op("dve", lambda e, i=i, sl=sl, tt=tt: e.scalar_tensor_tensor(tt[:, :], fT[:, i, sl], wsc[:, i:i + 1], rstd[:, sl], ALU.mult, ALU.mult), [r_t])
                    last = P.op("dve", lambda e, i=i, sl=sl, tt=tt: e.tensor_tensor(hT[:, i, sl], hT[:, i, sl], tt[:, :], ALU.add))
                marks["half"].append(last)
            return last
        for i in range(NKB):
            for half in range(2):
                sl = slice(half * 512, (half + 1) * 512)
                tt = tmpA if half == 0 else tmpB
                P.op("dve", lambda e, i=i, sl=sl, tt=tt: e.scalar_tensor_tensor(tt[:, :], fT[:, i, sl], wsc[:, i:i + 1], rstd[:, sl], ALU.mult, ALU.mult), [r_t])
                last = P.op("dve", lambda e, i=i, sl=sl, tt=tt: e.tensor_tensor(hT[:, i, sl], hT[:, i, sl], tt[:, :], ALU.add))
            upd_tok[i] = last
        return last

    def ffn(Wg, Wu, Wd, c_pre, c_post, deps=(), final=False):
        u_t = prenorm(c_pre, deps)
        if _SUB[0] == 1:
            return u_t
        ev_last = None
        for j in range(NFB):
            rg, ru = ("X", "Y") if j % 2 == 0 else ("W", "Z")
            pg = proj(Wg, [(0, 16)], j * 128, 128, u_rhs, rg, extra_deps=[u_t] if j == 0 else ())
            pu = proj(Wu, [(0, 16)], j * 128, 128, u_rhs, ru)
            for half in range(2):
                sl = slice(half * 512, (half + 1) * 512)
                tt = tmpA if half == 0 else tmpB
                a_t = P.op("act", lambda e, rg=rg, sl=sl, tt=tt: e.activation(tt[:, :], flat(PSR[rg])[:, sl], AF.Silu), [pg, ev_last])
                ev_last = P.op("dve", lambda e, ru=ru, sl=sl, tt=tt, j=j: e.tensor_tensor(HT[:, j, sl], tt[:, :], flat(PSR[ru])[:, sl], ALU.mult), [a_t, pu])
            psfree[rg] = a_t
            psfree[ru] = ev_last
        if _SUB[0] == 2:
            return [ev_last, a_t]
        return down_proj(Wd, NFB, HT, c_post, 0.5, extra_deps=[ev_last], final=final)

    fin_deps = [psfree[r] for r in grp_regions]
    if STOP >= 1:
        h1_t = ffn(w1g, w1u, w1d, P_F1PRE, P_F1POST, deps=fin_deps)
        fin_deps = [h1_t]
    if STOP >= 2:

        t_id2 = P.dma("sp", lambda e: e.dma_start(out=ident_f, in_=cident[:, :]), "m1", deps=[h1_t])
        t_mk = P.dma("sp", lambda e: e.dma_start(out=maskf, in_=cmask[:, :]), "m2", deps=[h1_t])
        t_wgk = P.dma("sp", lambda e: e.dma_start(out=wgk_f, in_=wgk[:, :]), "m3", deps=[h1_t])
        P.op("dve", lambda e: e.tensor_scalar(ident_b, ident_f, 1.0, None, ALU.mult), [t_wgk, t_id2, t_mk])
        t0 = P.op("dve", lambda e: e.tensor_scalar(wgk_b, wgk_f, 1.0, None, ALU.mult))
        um_t = prenorm(P_MPRE, [h1_t])
        pc = proj(w_in, [(0, 16)], C_CODE, 16, u_rhs, "X", extra_deps=[um_t], M=16)
        code_t = P.op("act", lambda e: e.activation(codeT, flat(PSR["X"])[0:16, :], AF.Identity), [pc])
        psfree["X"] = code_t

        chunk = lambda a, c: a[:, c * 64:(c + 1) * 64]
        hb = [code_t]
        cc_count = [0]

        def gate_slab(hidx):
            if hidx < 4:
                return proj(w_in, [(0, 16)], C_GK + hidx * 128, 128, u_rhs, "Y")
            return proj(w_in, [(0, 16)], C_HF + (hidx - 4) * 128, 128, u_rhs, "Y")

        def head_front(hidx, HB, qb_buf, pre_gate=None):
            is_gla = hidx < 4
            nv = 2 if is_gla else 1
            j = hidx if is_gla else hidx - 4
            cx = {"hidx": hidx, "is_gla": is_gla, "nv": nv, "j": j, "Qb": qb_buf}
            if is_gla:
                h = hidx
                pe_t = None
                for half in range(2):
                    o = flat(PSR["X"])[:, half * 512:(half + 1) * 512]
                    pe_t = P.op("pe", lambda e, o=o, h=h, half=half: e.matmul(o, wgk_b[:, h * 128:(h + 1) * 128], codeT[:, half * 512:(half + 1) * 512], start=True, stop=True),
                                [psfree["X"], t0, code_t] + HB)
                pk = pre_gate if pre_gate is not None else proj(w_in, [(0, 16)], C_GK + h * 128, 128, u_rhs, "Y", extra_deps=HB)
                a1 = P.op("act", lambda e, h=h: e.activation(EB, flat(PSR["X"]), AF.Exp, scale=-1.0, bias=nbgk[:, h:h + 1]), [pe_t, t_set] + HB)
                psfree["X"] = a1
                a2 = P.op("act", lambda e: e.activation(EB, EB, AF.Ln, bias=1.0))
                g_t = P.op("dve", lambda e: e.tensor_scalar(EA, EB, -1.0 / 16.0, None, ALU.mult), [a2] + HB)
                k_t = P.op("act", lambda e: e.copy(kTf, flat(PSR["Y"])), [pk] + HB)
                psfree["Y"] = k_t
                pq = proj(w_in, [(0, 16)], C_GQ + hidx * 128, 128, u_rhs, "X", extra_deps=HB)
            else:
                pf = pre_gate if pre_gate is not None else proj(w_in, [(0, 16)], C_HF + j * 128, 128, u_rhs, "Y", extra_deps=HB)
                pq = proj(w_in, [(0, 16)], C_HQ + j * 128, 128, u_rhs, "X", extra_deps=HB)
                a1 = P.op("act", lambda e: e.activation(EB, flat(PSR["Y"]), AF.Sigmoid, scale=-1.0), [pf, t_lb] + HB)
                psfree["Y"] = a1
                k_t = P.op("dve", lambda e, j=j: e.tensor_scalar(kTf, EB, lbt[:, 2, j:j + 1], None, ALU.mult), [a1] + HB)
                g_t = P.op("act", lambda e: e.activation(EA, kTf, AF.Ln, scale=-1.0, bias=1.0), [k_t])
            P.op("dve", lambda e: e.tensor_tensor_scan(Gt, onec, EA, 0.0, ALU.mult, ALU.add), [g_t] + HB)
            P.op("dve", lambda e: e.memset(Gb[:, 0:1], 0.0))
            P.op("dve", lambda e: e.tensor_scalar(Gb[:, 1:17], Gt[:, 63::64], 1.0, None, ALU.mult))
            G_t = P.op("dve", lambda e: e.tensor_tensor(dG, Gb[:, 1:17], Gb[:, 0:16], ALU.subtract))
            v3 = lambda a_: a_.rearrange("p (c d) -> p c d", c=16)
            bc = lambda a_: a_.unsqueeze(2).to_broadcast([128, 16, 64])
            gc_t = P.op("dve", lambda e: e.tensor_tensor(v3(EA), v3(Gt), bc(Gb[:, 0:16]), ALU.subtract))
            gh_t = P.op("dve", lambda e: e.tensor_tensor(v3(EC), v3(EA), bc(dG[:, 0:16]), ALU.subtract))
            P.op("act", lambda e: e.activation(dec, dG, AF.Exp), [G_t] + HB)
            P.op("act", lambda e: e.activation(EB, EA, AF.Exp, scale=-1.0), [gc_t])
            P.op("act", lambda e: e.activation(EC, EC, AF.Exp, scale=-1.0), [gh_t])
            P.op("act", lambda e: e.activation(EA, EA, AF.Exp))
            E_t = P.op("act", lambda e: e.activation(Gt, Gt, AF.Exp))
            P.op("dve", lambda e: e.tensor_tensor(Kt, kTf, EB, ALU.mult), [E_t, k_t])
            cx["kh_t"] = P.op("dve", lambda e: e.tensor_tensor(Kh, kTf, EC, ALU.mult))
            if is_gla:
                q_t = P.op("act", lambda e: e.mul(qf, flat(PSR["X"]), float(128 ** -0.5)), [pq])
            else:
                q_t = P.op("act", lambda e: e.activation(qf, flat(PSR["X"]), AF.Silu), [pq] + HB)
            psfree["X"] = q_t
            P.op("dve", lambda e: e.tensor_tensor(Qt, qf, EA, ALU.mult), [q_t])
            cx["qb_t"] = P.op("dve", lambda e: e.tensor_tensor(qb_buf, qf, Gt, ALU.mult))
            v_ts = []
            for blk in range(nv):
                vc = (C_GV + hidx * 256 + blk * 128) if is_gla else (C_HI + j * 128)
                reg = "Y" if blk == 0 else "X"
                pv = proj(w_in, [(0, 16)], vc, 128, u_rhs, reg, extra_deps=HB)
                v_t = P.op("act", lambda e, blk=blk, reg=reg: e.activation(vT[:, blk, :], flat(PSR[reg]), AF.Identity), [pv] + HB)
                psfree[reg] = v_t
                v_ts.append(v_t)
            cx["v_ts"] = v_ts
            return cx

        def head_back(cx, prev_fin, nxt=None):
            hidx, is_gla, nv, j = cx["hidx"], cx["is_gla"], cx["nv"], cx["j"]
            kh_t, qb_t, v_ts = cx["kh_t"], cx["qb_t"], cx["v_ts"]
            Zb = flat(PSR["Z"]).bitcast(BF16)
            Zb3 = Zb[0:64, :].rearrange("p (c d) -> p c d", c=16)
            Zf = flat(PSR["Z"]).rearrange("p (c d) -> p c d", c=8)
            pt = None
            for c in range(16):
                pt = P.op("pe", lambda e, c=c: e.transpose(Zb3[:, c, :], chunk(Kh, c), ident_b), [kh_t, psfree["Z"], t0] if c == 0 else [])
            kt_t = P.op("act", lambda e: e.activation(Ktok, Zb3, AF.Identity), [pt, qb_t])
            psfree["Z"] = kt_t
            st8 = {"vtok_free": None, "sb_free": None}

            def scan_state(blk):
                pt_ = None
                for c in range(16):
                    pt_ = P.op("pe", lambda e, c=c, blk=blk: e.transpose(Zb3[:, c, :], chunk(vT[:, blk, :], c), ident_b),
                               [v_ts[blk], psfree["Z"]] if c == 0 else [])
                vt_t = P.op("act", lambda e: e.activation(vtok, Zb3, AF.Identity), [pt_, st8["vtok_free"]])
                psfree["Z"] = vt_t
                Spp = [Sst2[:, blk, 0, :], Sst2[:, blk, 1, :]]
                s_t = P.op("dve", lambda e: e.memset(Spp[0], 0.0), [st8["sb_free"]])
                cp_prev = None
                for hh in range(2):
                    pp = None
                    for c8 in range(8):
                        c = hh * 8 + c8
                        pp = P.op("pe", lambda e, c=c, c8=c8: e.matmul(Zf[:, c8, :], Ktok[:, c, :], vtok[:, c, :], start=True, stop=True),
                                  [vt_t, kt_t, psfree["Z"]] if c8 == 0 else [])
                    if hh == 0 and blk == nv - 1 and nxt is not None:
                        cx["pre_gate"] = gate_slab(nxt)
                    for c8 in range(8):
                        c = hh * 8 + c8
                        cp = P.op("act", lambda e, c=c: e.activation(Sb[:, c, :], Spp[c % 2], AF.Identity), [s_t, st8["sb_free"]])
                        s_t = P.op("dve", lambda e, c=c, c8=c8: e.scalar_tensor_tensor(Spp[(c + 1) % 2], Spp[c % 2], dec[:, c:c + 1], Zf[:, c8, :], ALU.mult, ALU.add),
                                   [pp, cp_prev] if c8 == 0 else [cp_prev])
                        cp_prev = cp
                    psfree["Z"] = s_t
                st8["cp_last"] = cp_prev
                return s_t

            def scores():
                pa = None
                for c in range(16):
                    o = flat(PSR["W"])[0:64, c * 64:(c + 1) * 64]
                    pa = P.op("pe", lambda e, o=o, c=c: e.matmul(o, chunk(Kt, c), chunk(Qt, c), start=True, stop=True),
                              [qb_t, psfree["W"]] if c == 0 else [])
                Wv = flat(PSR["W"])[0:64, :].rearrange("p (c d) -> p c d", c=16)
                mb = maskf[:, :].unsqueeze(1).to_broadcast([64, 16, 64])
                a_t = P.op("dve", lambda e: e.tensor_tensor(A_sb.rearrange("p (c d) -> p c d", c=16), Wv, mb, ALU.mult), [pa, t_mk, kt_t])
                psfree["W"] = a_t
                return a_t

            def outputs(blk, s_tok, a_tok):
                po = None
                for c in range(16):
                    o = flat(PSR["W"])[:, c * 64:(c + 1) * 64]
                    P.op("pe", lambda e, o=o, c=c: e.matmul(o, vtok[:, c, :], chunk(A_sb, c), start=True, stop=False),
                         [psfree["W"], s_tok, a_tok, cps[blk]] if c == 0 else [])
                    po = P.op("pe", lambda e, o=o, c=c: e.matmul(o, Sb[:, c, :], chunk(Qt, c), start=False, stop=True))
                st8["vtok_free"] = po
                st8["sb_free"] = po
                o_t = P.op("act", lambda e, blk=blk: e.copy(o_raw[:, blk, :], flat(PSR["W"])), [po, prev_fin])
                psfree["W"] = o_t
                return po

            ncol = nv * 128
            srcd = cc_src[hidx].ap(); dstd = cc_dst[hidx].ap()

            def exchange(s_tok):
                d1 = P.dma("sp", lambda e: e.dma_start(out=srcd.rearrange("p (b d) -> p b d", b=nv), in_=Sst2[:, 0:nv, 0, :]), "cs", deps=[s_tok])
                cc_count[0] += 1
                P.wait("pool", d1)
                P.q["pool"].append(("cc", srcd, dstd))
                cct = ("cc", cc_count[0])
                return P.dma("sp", lambda e: e.dma_start(out=Sin[:, 0:ncol], in_=dstd[0:128, :]), "cl", deps=[cct])

            cps = {}
            if nv == 1:
                s_t = scan_state(0); cps[0] = st8["cp_last"]
                d2 = exchange(s_t)
                a_t = scores()
                po = outputs(0, s_t, a_t)
            else:
                s0 = scan_state(0); cps[0] = st8["cp_last"]
                a_t = scores()
                outputs(0, s0, a_t)
                s_t = scan_state(1); cps[1] = st8["cp_last"]
                d2 = exchange(s_t)
                po = outputs(1, s_t, a_t)
            gc = (C_GO + hidx * 256) if is_gla else (C_HO + j * 128)
            pg = proj(w_in, [(0, 16)], gc, 128, u_rhs, "X")
            g2 = P.op("act", lambda e: e.activation(sg, flat(PSR["X"]), AF.Silu), [pg, prev_fin])
            psfree["X"] = g2
            cx.update({"d2": d2, "g2": g2, "po": po, "s_t": s_t, "ncol": ncol})

        def head_tail(cx):
            hidx, is_gla, nv, j, ncol = cx["hidx"], cx["is_gla"], cx["nv"], cx["j"], cx["ncol"]
            qb_buf = cx["Qb"]
            si_t = P.op("dve", lambda e: e.tensor_scalar(Sin_b[:, 0:ncol], Sin[:, 0:ncol], par[:, P_FLAG:P_FLAG + 1], None, ALU.mult), [cx["d2"], t_par])
            ss_t = None
            for blk in range(nv):
                pcx = None
                for half in range(2):
                    o = flat(PSR["W"])[:, half * 512:(half + 1) * 512]
                    pcx = P.op("pe", lambda e, o=o, blk=blk, half=half: e.matmul(o, Sin_b[:, blk * 128:(blk + 1) * 128], qb_buf[:, half * 512:(half + 1) * 512], start=True, stop=True),
                               [si_t, psfree["W"]] if half == 0 else [])
                ad_t = P.op("dve", lambda e, blk=blk: e.tensor_tensor(o_raw[:, blk, :], o_raw[:, blk, :], flat(PSR["W"]), ALU.add), [pcx])
                psfree["W"] = ad_t
                b = 0
                sq_t = P.op("act", lambda e, blk=blk, b=b: e.activation(sqb[:, b, :], o_raw[:, blk, :], AF.Square), [ad_t, sq_free[b]])
                for half in range(2):
                    o = flat(PSR["Z"])[:, half * 512:(half + 1) * 512]
                    r = sqb[:, b, half * 512:(half + 1) * 512]
                    ss_t = P.op("pe", lambda e, o=o, r=r, st=(blk == 0), sp_=(blk == nv - 1): e.matmul(o, ones_b, r, start=st, stop=sp_),
                                [sq_t] + ([psfree["Z"]] if blk == 0 and half == 0 else []))
                sq_free[b] = ss_t
            r_t = rstd_from("Z", nv * 128, ss_t)
            fin = None
            g2 = cx["g2"]
            for blk in range(nv):
                wn = par[:, P_GNORM + blk:P_GNORM + blk + 1] if is_gla else par[:, P_HNORM:P_HNORM + 1]
                gblk = (hidx * 2 + blk) if is_gla else (8 + j)
                if blk == 1:
                    pg = proj(w_in, [(0, 16)], C_GO + hidx * 256 + 128, 128, u_rhs, "X")
                    g2 = P.op("act", lambda e: e.activation(sg, flat(PSR["X"]), AF.Silu), [pg, fin])
                    psfree["X"] = g2
                P.op("dve", lambda e, blk=blk, wn=wn: e.scalar_tensor_tensor(o_raw[:, blk, :], o_raw[:, blk, :], wn, rstd, ALU.mult, ALU.mult), [r_t, t_par])
                fin = P.op("dve", lambda e, blk=blk, gblk=gblk: e.tensor_tensor(oT_all[:, gblk, :], o_raw[:, blk, :], sg, ALU.mult), [g2])
            return fin

        fin_prev = code_t
        qb_bufs = [Qb, sqb[:, 1, :]]
        cx = head_front(0, [code_t], qb_bufs[0])
        for hidx in range(12):
            nxt = hidx + 1
            head_back(cx, fin_prev, nxt if nxt < 12 else None)
            if nxt < 12:
                cx_next = head_front(nxt, [cx["po"], cx["s_t"]], qb_bufs[nxt % 2], pre_gate=cx.get("pre_gate"))
                fin_prev = head_tail(cx)
                cx = cx_next
            else:
                fin_prev = head_tail(cx)
        hb[0] = fin_prev
        HB = list(hb)

        if _SUB[0] == 50:
            dl = None
            for i in range(NKB):
                dl = P.op("dve", lambda e, i=i: e.tensor_scalar(hT[:, i, :], oT_all[:, i, :], 1.0, None, ALU.mult), HB)
            fin_deps = [dl]
        else:
            mg_last = None
            tA = [tmpA, tmpB]
            tH = [rstd[:, 0:512], rstd[:, 512:1024]]
            for i in range(NKB):
                pzg = proj(w_in, [(0, 16)], C_ZG + i * 128, 128, u_rhs, "X", extra_deps=HB)
                pzh = proj(w_in, [(0, 16)], C_ZH + i * 128, 128, u_rhs, "Y")
                pbg = proj(wbg, [(0, 8)], i * 128, 128, lambda k, half: oT_all[:, k, half * 512:(half + 1) * 512], "W", extra_deps=HB)
                pbh = proj(wbh, [(0, 8)], i * 128, 128, lambda k, half: oT_all[:, 8 + k, half * 512:(half + 1) * 512], "Z")
                for half in range(2):
                    sl = slice(half * 512, (half + 1) * 512)
                    a1 = P.op("act", lambda e, i=i, sl=sl, half=half: e.activation(tA[half][:, :], flat(PSR["X"])[:, sl], AF.Sigmoid, bias=par[:, P_BB0 + i:P_BB0 + i + 1]), [pzg, mg_last, t_par])
                psfree["X"] = a1
                for half in range(2):
                    sl = slice(half * 512, (half + 1) * 512)
                    a2 = P.op("act", lambda e, i=i, sl=sl, half=half: e.activation(tH[half], flat(PSR["Y"])[:, sl], AF.Sigmoid, bias=par[:, P_BB1 + i:P_BB1 + i + 1]), [pzh, mg_last])
                psfree["Y"] = a2
                for half in range(2):
                    sl = slice(half * 512, (half + 1) * 512)
                    d1_ = P.op("dve", lambda e, sl=sl, half=half: e.tensor_tensor(tA[half][:, :], tA[half][:, :], flat(PSR["W"])[:, sl], ALU.mult), [a1, pbg])
                psfree["W"] = d1_
                for half in range(2):
                    sl = slice(half * 512, (half + 1) * 512)
                    d2_ = P.op("dve", lambda e, sl=sl, half=half: e.tensor_tensor(tH[half], tH[half], flat(PSR["Z"])[:, sl], ALU.mult), [a2, pbh])
                psfree["Z"] = d2_
                for half in range(2):
                    sl = slice(half * 512, (half + 1) * 512)
                    mg_last = P.op("dve", lambda e, i=i, sl=sl, half=half: e.tensor_tensor(mergedT[:, i, sl], tA[half][:, :], tH[half], ALU.add))
            h2_t = down_proj(w_out, NKB, mergedT, P_MPOST, 1.0, extra_deps=[mg_last])

            fin_deps = [h2_t]
            if STOP >= 3:
                h3_t = ffn(w2g, w2u, w2d, P_F2PRE, P_F2POST, deps=[h2_t], final=True)
                fin_deps = [h3_t]


    h3_t = fin_deps
    t_id3 = P.dma("sp", lambda e: e.dma_start(out=ident_f, in_=cident[:, :]), "m1", deps=[marks.get("dp_done", h3_t)] if STOP >= 1 and _SUB[0] == 9 else [h3_t])
    halves = marks.get("half") if (STOP >= 3 and _SUB[0] == 9) else None
    os_free = [None, None]
    st_toks = []
    for t in range(8):
        sbuf = ostage[t]
        evs = []
        for g in range(4):
            reg = grp_regions[g]
            pe_t = None
            for q4 in range(4):
                i = g * 4 + q4
                o = flat(PSR[reg])[:, q4 * 128:(q4 + 1) * 128]
                inn = hT[:, i, t * 128:(t + 1) * 128]
                hdep = halves[t // 4] if halves else h3_t
                deps = [hdep, t_id3, psfree[reg]] if q4 == 0 else []
                pe_t = P.op("pe", lambda e, o=o, inn=inn: e.transpose(o, inn, ident_f), deps)
            dst = sbuf[:, g * 512:(g + 1) * 512]
            srcv = flat(PSR[reg])[:, 0:512]
            if g % 2 == 0 or (halves and t < 4):
                ev = P.op("act", lambda e, dst=dst, srcv=srcv: e.copy(dst, srcv), [pe_t])
            else:
                ev = P.op("dve", lambda e, dst=dst, srcv=srcv: e.tensor_copy(dst, srcv), [pe_t])
            psfree[reg] = ev
            evs.append(ev)
        stt = P.dma("sp", lambda e, sbuf=sbuf, t=t: e.dma_start(out=out[t * 128:(t + 1) * 128, :], in_=sbuf), f"os{t}", deps=evs)
        os_free[t % 2] = stt
        st_toks.append(stt)
    P.wait("sp", st_toks)

    sem_names = sorted(set(P.dcnt.keys()))
    sems = {}
    for e in ("pe", "act", "dve"):
        sems[e] = nc.alloc_semaphore(name=f"ms_{e}")
    for s in sem_names:
        sems[s] = nc.alloc_semaphore(name=f"d_{s}")
    sems["cc"] = nc.alloc_semaphore(name="ccsem")

    for e in ("pe", "act", "dve"):
        n = 0
        for item in P.q[e]:
            if item[0] == "op" and item[2].used:
                n += 1
                item[2].ms = n

    def replay(ename, eh):
        seen = {}
        for item in P.q[ename]:
            kind = item[0]
            if kind == "wait":
                tok = item[1]
                if isinstance(tok, Tok):
                    key, val = tok.eng, tok.ms
                else:
                    key, val = tok
                if seen.get(key, 0) >= val:
                    continue
                seen[key] = val
                eh.wait_ge(sems[key], val)
            elif kind == "op":
                ins = item[1](eh)
                if item[2].used:
                    ins.then_inc(sems[ename], 1)
            elif kind == "dma":
                item[1](eh).then_inc(sems[item[2]], 16)
            elif kind == "cc":
                eh.collective_compute("AllGather", ALU.bypass, replica_groups=[[0, 1], [2, 3], [4, 5], [6, 7]],
                                      ins=[item[1].opt()], outs=[item[2].opt()]).then_inc(sems["cc"])

    with nc.Block() as block:
        @block.tensor
        def _(e):
            replay("pe", e)

        @block.scalar
        def _(e):
            replay("act", e)

        @block.vector
        def _(e):
            replay("dve", e)

        @block.gpsimd
        def _(e):
            replay("pool", e)

        @block.sync
        def _(e):
            replay("sp", e)
    nc._declared = declared
    return nc


_NC = [None]
_STOP = [3]
_SUB = [9]


def kernel(**inputs):
    f32 = lambda a: np.ascontiguousarray(np.asarray(a, dtype=np.float32))
    x = f32(inputs["x"])
    fm = lambda v, n: f32(v).reshape(n, 128).T
    par = np.zeros((128, NPAR), np.float32)
    par[:, P_F1PRE:P_F1PRE + 16] = fm(inputs["ffn1_pre_norm"][0], 16)
    par[:, P_F1POST:P_F1POST + 16] = fm(inputs["ffn1_post_norm"][0], 16)
    par[:, P_MPRE:P_MPRE + 16] = fm(inputs["mix_pre_norm"][0], 16)
    par[:, P_MPOST:P_MPOST + 16] = fm(inputs["mix_post_norm"][0], 16)
    par[:, P_F2PRE:P_F2PRE + 16] = fm(inputs["ffn2_pre_norm"][0], 16)
    par[:, P_F2POST:P_F2POST + 16] = fm(inputs["ffn2_post_norm"][0], 16)
    par[:, P_BGK:P_BGK + 4] = fm(inputs["gla_b_gk"][0], 4)
    par[:, P_GNORM:P_GNORM + 2] = fm(inputs["gla_norm"][0], 2)
    par[:, P_HNORM:P_HNORM + 1] = fm(inputs["hgrn_norm"][0], 1)
    par[:, P_L0:P_L0 + 8] = fm(inputs["hgrn_lb_logits"][0], 8)
    par[:, P_L1:P_L1 + 8] = fm(inputs["hgrn_lb_logits"][1], 8)
    par[:, P_BB0:P_BB0 + 16] = fm(inputs["b_branch_gates"][0, 0], 16)
    par[:, P_BB1:P_BB1 + 16] = fm(inputs["b_branch_gates"][0, 1], 16)
    ident = np.eye(128, dtype=np.float32)
    mask = np.triu(np.ones((64, 64), np.float32))
    shared = {
        "c_ident": ident, "c_mask": mask,
        "ffn1_w_gate": f32(inputs["ffn1_w_gate"][0][:, :DFF]), "ffn1_w_up": f32(inputs["ffn1_w_up"][0][:, :DFF]),
        "ffn1_w_down": f32(inputs["ffn1_w_down"][0][:DFF]),
        "ffn2_w_gate": f32(inputs["ffn2_w_gate"][0][:, :DFF]), "ffn2_w_up": f32(inputs["ffn2_w_up"][0][:, :DFF]),
        "ffn2_w_down": f32(inputs["ffn2_w_down"][0][:DFF]),
        "w_in": f32(inputs["w_in"][0]), "gla_w_gk_up": f32(inputs["gla_w_gk_up"][0]),
        "w_branch_gla": f32(inputs["w_branch_gla"][0]), "w_branch_hgrn": f32(inputs["w_branch_hgrn"][0]),
        "w_out": f32(inputs["w_out"][0]),
    }
    if _NC[0] is None:
        _NC[0] = build_program()
    shared = {k: v for k, v in shared.items() if k in _NC[0]._declared}
    in_maps = []
    for c in range(8):
        b, s = c // 2, c % 2
        p = par.copy()
        p[:, P_FLAG] = float(s)
        m = dict(shared)
        m["x"] = np.ascontiguousarray(x[b, s * T:(s + 1) * T, :])
        m["params"] = p
        in_maps.append(m)
    if _NC[0] is None:
        _NC[0] = build_program()
    res = run_bass_kernel_spmd(_NC[0], in_maps, core_ids=list(range(8)))
    outp = np.empty((4, 2048, D), np.float32)
    for c in range(8):
        b, s = c // 2, c % 2
        outp[b, s * T:(s + 1) * T, :] = res.results[c]["out"]
    return outp
```

```python
import numpy as np
import concourse.bass as bass
import concourse.mybir as mybir
from concourse.bass_utils import run_bass_kernel_spmd

F32 = mybir.dt.float32
BF16 = mybir.dt.bfloat16
U8 = mybir.dt.uint8
AF = mybir.ActivationFunctionType
ALU = mybir.AluOpType

D = 2048
T = 1024
NKB = 16
DFF = 5504
NFB = 43
INW = 11280
EPS = 1e-6
NB = 3
C_GQ, C_GK, C_GV, C_GO, C_CODE = 0, 512, 1024, 2048, 3072
C_HQ, C_HF, C_HI, C_HO, C_ZG, C_ZH = 3088, 4112, 5136, 6160, 7184, 9232

P_F1PRE, P_F1POST, P_MPRE, P_MPOST, P_F2PRE, P_F2POST = 0, 16, 32, 48, 64, 80
P_BGK, P_GNORM, P_HNORM, P_L0, P_L1, P_BB0, P_BB1, P_FLAG = 96, 100, 102, 103, 111, 119, 135, 151
NPAR = 152


class Tok:
    __slots__ = ("eng", "idx", "used", "ms")

    def __init__(self, eng, idx):
        self.eng, self.idx, self.used, self.ms = eng, idx, False, None


class Prog:
    ENGS = ("pe", "act", "dve", "pool", "sp")

    def __init__(self):
        self.q = {e: [] for e in self.ENGS}
        self.dcnt = {}
        self.last = {}

    def wait(self, eng, tok):
        if tok is None:
            return
        if isinstance(tok, (list, tuple)) and not (len(tok) == 2 and isinstance(tok[0], str)):
            for t in tok:
                self.wait(eng, t)
            return
        if isinstance(tok, Tok):
            if tok.eng == eng:
                return
            tok.used = True
        self.q[eng].append(("wait", tok))

    def op(self, eng, fn, deps=()):
        self.wait(eng, deps)
        if eng in ("act", "dve"):
            prev = self.last.get(eng)
            if prev is not None:
                prev.used = True
                self.q[eng].append(("wait", prev))
        tok = Tok(eng, len(self.q[eng]))
        self.q[eng].append(("op", fn, tok))
        self.last[eng] = tok
        return tok

    def dma(self, eng, fn, semkey, deps=(), inc=16):
        self.wait(eng, deps)
        self.dcnt[semkey] = self.dcnt.get(semkey, 0) + inc
        tok = (semkey, self.dcnt[semkey])
        self.q[eng].append(("dma", fn, semkey))
        return tok


def build_program():
    nc = bass.Bass("TRN2", target_bir_lowering=False)
    STOP = _STOP[0]
    declared = []

    def dt_in(name, shape, need=0):
        if STOP < need:
            return None
        declared.append(name)
        return nc.dram_tensor(name, shape, F32, kind="ExternalInput").ap()
    x = dt_in("x", [T, D])
    params = dt_in("params", [128, NPAR])
    cident = dt_in("c_ident", [128, 128])
    cmask = dt_in("c_mask", [64, 64])
    w1g = dt_in("ffn1_w_gate", [D, DFF], 1); w1u = dt_in("ffn1_w_up", [D, DFF], 1); w1d = dt_in("ffn1_w_down", [DFF, D], 1)
    w2g = dt_in("ffn2_w_gate", [D, DFF], 3); w2u = dt_in("ffn2_w_up", [D, DFF], 3); w2d = dt_in("ffn2_w_down", [DFF, D], 3)
    w_in = dt_in("w_in", [D, INW], 2)
    wgk = dt_in("gla_w_gk_up", [16, 512], 2)
    wbg = dt_in("w_branch_gla", [1024, D], 2); wbh = dt_in("w_branch_hgrn", [1024, D], 2)
    w_out = dt_in("w_out", [D, D], 2)
    out = nc.dram_tensor("out", [T, D], F32, kind="ExternalOutput").ap()
    cc_src = [nc.dram_tensor(f"cc_src{h}", [128, 256 if h < 4 else 128], F32) for h in range(12)]
    cc_dst = [nc.dram_tensor(f"cc_dst{h}", [256, 256 if h < 4 else 128], F32) for h in range(12)]

    P = Prog()
    ARENA = 212800
    arena_h = nc.alloc_sbuf_tensor("arena", [128, ARENA], U8)
    arena = arena_h.ap() if hasattr(arena_h, "ap") else arena_h
    ps_h = nc.alloc_psum_tensor("ps", [128, 8, 512], F32)
    ps = ps_h.ap() if hasattr(ps_h, "ap") else ps_h

    off = [0]

    def carve(nbytes, dtype, parts=128, at=None):
        if at is None:
            a = off[0]
            off[0] += (nbytes + 31) // 32 * 32
            assert off[0] <= ARENA, off[0]
        else:
            a = at
        return arena[0:parts, a:a + nbytes].bitcast(dtype)

    hT = carve(65536, F32).rearrange("p (i t) -> p i t", i=NKB)
    uT_off = off[0]
    uT = carve(32768, BF16).rearrange("p (i t) -> p i t", i=NKB)
    fT = uT
    big_off = off[0]
    HT = carve(88064, BF16).rearrange("p (i t) -> p i t", i=NFB)
    wring = carve(NB * 4096, BF16).rearrange("p (b k c) -> p b k c", b=NB, k=16)
    rstd = carve(4096, F32)
    sqb = carve(4096, BF16).rearrange("p (b t) -> p b t", b=2)
    tmp_off = off[0]
    tmpA = carve(2048, F32)
    tmpB = carve(2048, F32)
    par = carve(NPAR * 4, F32)
    ones_b = carve(256, BF16)
    tl = big_off + 80896
    ident_f = carve(512, F32, at=tl)
    maskf = carve(256, F32, parts=64, at=tl + 512)
    wgk_f = carve(2048, F32, parts=16, at=tl + 768)
    wgk_b = carve(1024, BF16, parts=16, at=tl + 2816)
    ident_b = carve(256, BF16, at=tl + 3840)
    lbt = carve(8 * 4 * 4, F32).rearrange("p (a j) -> p a j", a=4)
    nbgk = carve(16, F32)
    one1 = carve(4, F32)
    onec = one1[:, 0:1].to_broadcast([128, 1024])
    xstage = [carve(8192, F32, at=big_off + i * 8192) for i in range(8)]
    oT_all = carve(32768, BF16, at=big_off).rearrange("p (i t) -> p i t", i=NKB)
    codeT = carve(2048, BF16, parts=16, at=big_off + 32768)
    tb = big_off + 34816
    mergedT = carve(32768, BF16, at=tb).rearrange("p (i t) -> p i t", i=NKB)
    Qt = carve(2048, BF16, at=tb + 0)
    Qb = carve(2048, BF16, at=tb + 2048)
    Kt = carve(2048, BF16, at=tb + 4096)
    Kh = carve(2048, BF16, at=tb + 6144)
    vT = carve(4096, BF16, at=tb + 8192).rearrange("p (b t) -> p b t", b=2)
    pb = tb + 12288
    Gt = carve(4096, F32, at=pb)
    EA = carve(4096, F32, at=pb + 4096)
    EB = carve(4096, F32, at=pb + 8192)
    EC = carve(4096, F32, at=pb + 12288)
    kTf = carve(4096, F32, at=pb + 16384)
    qf = carve(4096, F32, at=pb + 20480)
    A_sb = carve(2048, BF16, parts=64, at=pb + 4096)
    Ktok = carve(4096, BF16, parts=64, at=pb + 6144).rearrange("p (c d) -> p c d", c=16)
    vtok = carve(4096, BF16, parts=64, at=pb + 10240).rearrange("p (c d) -> p c d", c=16)
    Sb = carve(4096, BF16, at=pb + 14336).rearrange("p (c d) -> p c d", c=16)
    sg = carve(4096, F32, at=tmp_off)
    Sst2 = carve(2048, F32, at=tl + 4096).rearrange("p (b q d) -> p b q d", b=2, q=2)
    Sin = carve(1024, F32, at=tl + 6144)
    sm = pb + 24576
    Sin_b = carve(512, BF16, at=sm)
    Gb = carve(68, F32, at=sm + 512)
    nGb = carve(68, F32, at=sm + 640)
    dG = carve(64, F32, at=sm + 768)
    dec = carve(64, F32, at=sm + 896)
    o_raw = carve(8192, F32, at=sm + 1024).rearrange("p (b t) -> p b t", b=2)
    assert sm + 1024 + 8192 <= big_off + 88064
    ostage = [carve(8192, F32, at=big_off + i * 8192) for i in range(8)]

    PSR = {"X": ps[:, 0:2, :], "Y": ps[:, 2:4, :], "W": ps[:, 4:6, :], "Z": ps[:, 6:8, :]}
    flat = lambda a: a.rearrange("p a b -> p (a b)")
    psfree = {k: None for k in PSR}

    wstate = {"i": 0, "free": [None] * NB}

    def slab(src, k0, nk, c0, ncols):
        i = wstate["i"]; wstate["i"] += 1
        b = i % NB
        dst = wring[:, b, 0:nk, 0:ncols]
        s = src[k0 * 128:(k0 + nk) * 128, c0:c0 + ncols].rearrange("(k p) c -> p k c", p=128)
        tok = P.dma("pool", lambda e, dst=dst, s=s: e.dma_start(out=dst, in_=s), f"wld{b}",
                    deps=[wstate["free"][b]])
        return wring[:, b], tok, b

    def proj(src, k0s, c0, ncols, rhs_fn, region, extra_deps=(), M=128):
        ktot = sum(nk for _, nk in k0s)
        kk = 0
        last = None
        first = True
        for (k0, nk) in k0s:
            wv, ltok, b = slab(src, k0, nk, c0, ncols)
            for k in range(nk):
                for half in range(2):
                    deps = []
                    if first:
                        deps = [ltok, psfree[region]] + list(extra_deps)
                    elif k == 0 and half == 0:
                        deps = [ltok]
                    first = False
                    o = flat(PSR[region])[0:M, half * 512:(half + 1) * 512]
                    l = wv[:, k, 0:ncols]
                    r = rhs_fn(k0 + k, half)
                    st, sp_ = (kk == 0), (kk == ktot - 1)
                    last = P.op("pe", lambda e, o=o, l=l, r=r, st=st, sp_=sp_: e.matmul(o, l, r, start=st, stop=sp_), deps)
                kk += 1
            wstate["free"][b] = last
        return last

    u_rhs = lambda k, half: uT[:, k, half * 512:(half + 1) * 512]

    sp_tok = {}
    t_par = P.dma("sp", lambda e: e.dma_start(out=par, in_=params[:, :]), "m0")
    t_id = P.dma("sp", lambda e: e.dma_start(out=ident_f, in_=cident[:, :]), "m1")
    P.op("dve", lambda e: e.memset(ones_b, 1.0), [t_par])
    wsc_all = {}
    for (wcol_, fac_) in ((P_F1POST, 0.5), (P_MPOST, 1.0), (P_F2POST, 0.5)):
        w_ = carve(64, F32)
        P.op("dve", lambda e, w_=w_, wcol_=wcol_, fac_=fac_: e.tensor_scalar(w_, par[:, wcol_:wcol_ + 16], float(fac_), None, ALU.mult))
        wsc_all[(wcol_, float(fac_))] = w_
    P.op("dve", lambda e: e.memset(one1, 1.0))
    P.op("dve", lambda e: e.tensor_tensor(lbt[:, 0, :], par[:, P_L0:P_L0 + 8], par[:, P_L1:P_L1 + 8], ALU.subtract))
    t_set = P.op("dve", lambda e: e.tensor_scalar(nbgk, par[:, P_BGK:P_BGK + 4], -1.0, None, ALU.mult))
    P.op("act", lambda e: e.activation(lbt[:, 1, :], lbt[:, 0, :], AF.Sigmoid), [t_set])
    t_lb = P.op("act", lambda e: e.activation(lbt[:, 2, :], lbt[:, 0, :], AF.Sigmoid, scale=-1.0))

    xs_free = [None, None]
    grp_regions = ["X", "Y", "W", "Z"]
    xts = [P.dma("sp", lambda e, t=t: e.dma_start(out=xstage[t], in_=x[t * 128:(t + 1) * 128, :]), f"xl{t}") for t in range(8)]
    grp_done = {}
    for g in range(4):
        lastA = lastD = None
        for t in range(8):
            sbuf = xstage[t]
            reg = grp_regions[t % 4]
            pe_t = None
            for q4 in range(4):
                i = g * 4 + q4
                o = flat(PSR[reg])[:, q4 * 128:(q4 + 1) * 128]
                inn = sbuf[:, i * 128:(i + 1) * 128]
                deps = [xts[t], t_id, psfree[reg]] if q4 == 0 else []
                pe_t = P.op("pe", lambda e, o=o, inn=inn: e.transpose(o, inn, ident_f), deps)
            dst = hT[:, g * 4:(g + 1) * 4, t * 128:(t + 1) * 128]
            srcv = flat(PSR[reg])[:, 0:512].rearrange("p (a b) -> p a b", a=4)
            if t % 2 == 0:
                lastA = psfree[reg] = P.op("act", lambda e, dst=dst, srcv=srcv: e.copy(dst, srcv), [pe_t])
            else:
                lastD = psfree[reg] = P.op("dve", lambda e, dst=dst, srcv=srcv: e.tensor_copy(dst, srcv), [pe_t])
        grp_done[g] = [lastA, lastD]

    sq_free = [None, None]

    def sumsq_accumulate(src_fn, nblk, region="W", src_deps=(), split=False):
        last = None
        for i in range(nblk):
            b = i % 2
            deps_i = src_deps[i] if isinstance(src_deps, dict) else src_deps
            if split and b == 1:
                s_t = P.op("dve", lambda e, i=i, b=b: e.tensor_tensor(sqb[:, b, :], src_fn(i), src_fn(i), ALU.mult),
                           [sq_free[b]] + list(deps_i if deps_i else []))
            else:
                s_t = P.op("act", lambda e, i=i, b=b: e.activation(sqb[:, b, :], src_fn(i), AF.Square),
                           [sq_free[b]] + list(deps_i if deps_i else []))
            for half in range(2):
                o = flat(PSR[region])[:, half * 512:(half + 1) * 512]
                r = sqb[:, b, half * 512:(half + 1) * 512]
                deps = [s_t] + ([psfree[region]] if i == 0 and half == 0 else [])
                last = P.op("pe", lambda e, o=o, r=r, st=(i == 0), sp_=(i == nblk - 1): e.matmul(o, ones_b, r, start=st, stop=sp_), deps)
            sq_free[b] = last
        return last

    def rstd_from(region, n, pe_tok):
        a = P.op("act", lambda e: e.activation(rstd, flat(PSR[region]), AF.Ln, scale=1.0 / n, bias=EPS_AP), [pe_tok, t_eps])
        psfree[region] = a
        return P.op("act", lambda e: e.activation(rstd, rstd, AF.Exp, scale=-0.5))

    eps_t = carve(4, F32)
    EPS_AP = eps_t[:, 0:1]
    t_eps = P.op("dve", lambda e: e.memset(eps_t, EPS))

    def prenorm(wcol, deps=()):
        sd = {i: (upd_tok[i] if isinstance(upd_tok[i], list) else [upd_tok[i]]) for i in range(NKB)} if (upd_tok and deps) else list(deps)
        pe_tok = sumsq_accumulate(lambda i: hT[:, i, :], NKB, "W", src_deps=sd, split=not isinstance(sd, dict))
        r_t = rstd_from("W", D, pe_tok)
        last = None
        for i in range(NKB):
            last = P.op("dve", lambda e, i=i: e.scalar_tensor_tensor(uT[:, i, :], hT[:, i, :], par[:, wcol + i:wcol + i + 1], rstd, ALU.mult, ALU.mult),
                        [t_par, r_t])
        return last

    upd_tok = {}
    marks = {}

    def down_proj(W, nkb, rhsT, wcol, factor, extra_deps=(), final=False):
        pieces = []
        k0 = 0
        while k0 < nkb:
            nk = min(16, nkb - k0)
            pieces.append((k0, nk)); k0 += nk
        ss_last = None
        for i in range(NKB):
            reg = "X" if i % 2 == 0 else "Y"
            pe_t = proj(W, pieces, i * 128, 128, lambda k, half: rhsT[:, k, half * 512:(half + 1) * 512], reg,
                        extra_deps=extra_deps if i == 0 else ())
            b = i % 2
            for half in range(2):
                sl = slice(half * 512, (half + 1) * 512)
                if half == 0:
                    c_t = P.op("dve", lambda e, i=i, reg=reg, sl=sl: e.tensor_scalar(fT[:, i, sl], flat(PSR[reg])[:, sl], 1.0, None, ALU.mult), [pe_t])
                else:
                    c_t = P.op("dve", lambda e, i=i, reg=reg, sl=sl: e.tensor_scalar(fT[:, i, sl], flat(PSR[reg])[:, sl], 1.0, None, ALU.mult), [pe_t])
                s_t = c_t if half == 1 else c_t
                if half == 0:
                    c0_t = c_t
            psfree[reg] = [c0_t, c_t]
        marks["dp_done"] = pe_t
        ss_last = sumsq_accumulate(lambda i: fT[:, i, :], NKB, "W", src_deps=[c_t], split=True)
        r_t = rstd_from("W", D, ss_last)
        if _SUB[0] == 3:
            return [r_t, c_t]
        wsc = wsc_all[(wcol, float(factor))]
        last = None
        if final:
            marks["half"] = []
            for half in range(2):
                sl = slice(half * 512, (half + 1) * 512)
                for i in range(NKB):
                    tt = tmpA if i % 2 == 0 else tmpB
                    P.op("dve", lambda e, i=i, sl=sl, tt=tt: e.scalar_tensor_tensor(tt[:, :], fT[:, i, sl], wsc[:, i:i + 1], rstd[:, sl], ALU.mult, ALU.mult), [r_t])
                    last = P.op("dve", lambda e, i=i, sl=sl, tt=tt: e.tensor_tensor(hT[:, i, sl], hT[:, i, sl], tt[:, :], ALU.add))
                marks["half"].append(last)
            return last
        for i in range(NKB):
            for half in range(2):
                sl = slice(half * 512, (half + 1) * 512)
                tt = tmpA if half == 0 else tmpB
                P.op("dve", lambda e, i=i, sl=sl, tt=tt: e.scalar_tensor_tensor(tt[:, :], fT[:, i, sl], wsc[:, i:i + 1], rstd[:, sl], ALU.mult, ALU.mult), [r_t])
                last = P.op("dve", lambda e, i=i, sl=sl, tt=tt: e.tensor_tensor(hT[:, i, sl], hT[:, i, sl], tt[:, :], ALU.add))
            upd_tok[i] = last
        return last

    def ffn(Wg, Wu, Wd, c_pre, c_post, deps=(), final=False):
        u_t = prenorm(c_pre, deps)
        if _SUB[0] == 1:
            return u_t
        ev_last = None
        for j in range(NFB):
            rg, ru = ("X", "Y") if j % 2 == 0 else ("W", "Z")
            pg = proj(Wg, [(0, 16)], j * 128, 128, u_rhs, rg, extra_deps=[u_t] if j == 0 else ())
            pu = proj(Wu, [(0, 16)], j * 128, 128, u_rhs, ru)
            for half in range(2):
                sl = slice(half * 512, (half + 1) * 512)
                tt = tmpA if half == 0 else tmpB
                a_t = P.op("act", lambda e, rg=rg, sl=sl, tt=tt: e.activation(tt[:, :], flat(PSR[rg])[:, sl], AF.Silu), [pg, ev_last])
                ev_last = P.op("dve", lambda e, ru=ru, sl=sl, tt=tt, j=j: e.tensor_tensor(HT[:, j, sl], tt[:, :], flat(PSR[ru])[:, sl], ALU.mult), [a_t, pu])
            psfree[rg] = a_t
            psfree[ru] = ev_last
        if _SUB[0] == 2:
            return [ev_last, a_t]
        return down_proj(Wd, NFB, HT, c_post, 0.5, extra_deps=[ev_last], final=final)

    fin_deps = [psfree[r] for r in grp_regions]
    if STOP >= 1:
        for i_ in range(NKB):
            upd_tok[i_] = grp_done[i_ // 4]
        h1_t = ffn(w1g, w1u, w1d, P_F1PRE, P_F1POST, deps=fin_deps)
        fin_deps = [h1_t]
    if STOP >= 2:

        t_id2 = P.dma("sp", lambda e: e.dma_start(out=ident_f, in_=cident[:, :]), "m1", deps=[h1_t])
        t_mk = P.dma("sp", lambda e: e.dma_start(out=maskf, in_=cmask[:, :]), "m2", deps=[h1_t])
        t_wgk = P.dma("sp", lambda e: e.dma_start(out=wgk_f, in_=wgk[:, :]), "m3", deps=[h1_t])
        P.op("dve", lambda e: e.tensor_scalar(ident_b, ident_f, 1.0, None, ALU.mult), [t_wgk, t_id2, t_mk])
        t0 = P.op("dve", lambda e: e.tensor_scalar(wgk_b, wgk_f, 1.0, None, ALU.mult))
        um_t = prenorm(P_MPRE, [h1_t])
        pc = proj(w_in, [(0, 16)], C_CODE, 16, u_rhs, "X", extra_deps=[um_t], M=16)
        code_t = P.op("act", lambda e: e.activation(codeT, flat(PSR["X"])[0:16, :], AF.Identity), [pc])
        psfree["X"] = code_t

        chunk = lambda a, c: a[:, c * 64:(c + 1) * 64]
        hb = [code_t]
        cc_count = [0]

        def gate_slab(hidx):
            if hidx < 4:
                return proj(w_in, [(0, 16)], C_GK + hidx * 128, 128, u_rhs, "Y")
            return proj(w_in, [(0, 16)], C_HF + (hidx - 4) * 128, 128, u_rhs, "Y")

        def head_front(hidx, HB, qb_buf, pre_gate=None):
            is_gla = hidx < 4
            nv = 2 if is_gla else 1
            j = hidx if is_gla else hidx - 4
            cx = {"hidx": hidx, "is_gla": is_gla, "nv": nv, "j": j, "Qb": qb_buf}
            if is_gla:
                h = hidx
                pe_t = None
                for half in range(2):
                    o = flat(PSR["X"])[:, half * 512:(half + 1) * 512]
                    pe_t = P.op("pe", lambda e, o=o, h=h, half=half: e.matmul(o, wgk_b[:, h * 128:(h + 1) * 128], codeT[:, half * 512:(half + 1) * 512], start=True, stop=True),
                                [psfree["X"], t0, code_t] + HB)
                pk = pre_gate if pre_gate is not None else proj(w_in, [(0, 16)], C_GK + h * 128, 128, u_rhs, "Y", extra_deps=HB)
                a1 = P.op("act", lambda e, h=h: e.activation(EB, flat(PSR["X"]), AF.Exp, scale=-1.0, bias=nbgk[:, h:h + 1]), [pe_t, t_set] + HB)
                psfree["X"] = a1
                a2 = P.op("act", lambda e: e.activation(EB, EB, AF.Ln, bias=1.0))
                g_t = P.op("dve", lambda e: e.tensor_scalar(EA, EB, -1.0 / 16.0, None, ALU.mult), [a2] + HB)
                k_t = P.op("act", lambda e: e.copy(kTf, flat(PSR["Y"])), [pk] + HB)
                psfree["Y"] = k_t
                pq = proj(w_in, [(0, 16)], C_GQ + hidx * 128, 128, u_rhs, "X", extra_deps=HB)
            else:
                pf = pre_gate if pre_gate is not None else proj(w_in, [(0, 16)], C_HF + j * 128, 128, u_rhs, "Y", extra_deps=HB)
                pq = proj(w_in, [(0, 16)], C_HQ + j * 128, 128, u_rhs, "X", extra_deps=HB)
                a1 = P.op("act", lambda e: e.activation(EB, flat(PSR["Y"]), AF.Sigmoid, scale=-1.0), [pf, t_lb] + HB)
                psfree["Y"] = a1
                k_t = P.op("dve", lambda e, j=j: e.tensor_scalar(kTf, EB, lbt[:, 2, j:j + 1], None, ALU.mult), [a1] + HB)
                g_t = P.op("act", lambda e: e.activation(EA, kTf, AF.Ln, scale=-1.0, bias=1.0), [k_t])
            P.op("dve", lambda e: e.tensor_tensor_scan(Gt, onec, EA, 0.0, ALU.mult, ALU.add), [g_t] + HB)
            P.op("dve", lambda e: e.memset(Gb[:, 0:1], 0.0))
            P.op("dve", lambda e: e.tensor_scalar(Gb[:, 1:17], Gt[:, 63::64], 1.0, None, ALU.mult))
            G_t = P.op("dve", lambda e: e.tensor_tensor(dG, Gb[:, 1:17], Gb[:, 0:16], ALU.subtract))
            v3 = lambda a_: a_.rearrange("p (c d) -> p c d", c=16)
            bc = lambda a_: a_.unsqueeze(2).to_broadcast([128, 16, 64])
            gc_t = P.op("dve", lambda e: e.tensor_tensor(v3(EA), v3(Gt), bc(Gb[:, 0:16]), ALU.subtract))
            gh_t = P.op("dve", lambda e: e.tensor_tensor(v3(EC), v3(EA), bc(dG[:, 0:16]), ALU.subtract))
            P.op("act", lambda e: e.activation(dec, dG, AF.Exp), [G_t] + HB)
            P.op("act", lambda e: e.activation(EB, EA, AF.Exp, scale=-1.0), [gc_t])
            P.op("act", lambda e: e.activation(EC, EC, AF.Exp, scale=-1.0), [gh_t])
            P.op("act", lambda e: e.activation(EA, EA, AF.Exp))
            E_t = P.op("act", lambda e: e.activation(Gt, Gt, AF.Exp))
            P.op("dve", lambda e: e.tensor_tensor(Kt, kTf, EB, ALU.mult), [E_t, k_t])
            cx["kh_t"] = P.op("dve", lambda e: e.tensor_tensor(Kh, kTf, EC, ALU.mult))
            if is_gla:
                q_t = P.op("act", lambda e: e.mul(qf, flat(PSR["X"]), float(128 ** -0.5)), [pq])
            else:
                q_t = P.op("act", lambda e: e.activation(qf, flat(PSR["X"]), AF.Silu), [pq] + HB)
            psfree["X"] = q_t
            P.op("dve", lambda e: e.tensor_tensor(Qt, qf, EA, ALU.mult), [q_t])
            cx["qb_t"] = P.op("dve", lambda e: e.tensor_tensor(qb_buf, qf, Gt, ALU.mult))
            v_ts = []
            for blk in range(nv):
                vc = (C_GV + hidx * 256 + blk * 128) if is_gla else (C_HI + j * 128)
                reg = "Y" if blk == 0 else "X"
                pv = proj(w_in, [(0, 16)], vc, 128, u_rhs, reg, extra_deps=HB)
                v_t = P.op("act", lambda e, blk=blk, reg=reg: e.activation(vT[:, blk, :], flat(PSR[reg]), AF.Identity), [pv] + HB)
                psfree[reg] = v_t
                v_ts.append(v_t)
            cx["v_ts"] = v_ts
            return cx

        def head_back(cx, prev_fin, nxt=None):
            hidx, is_gla, nv, j = cx["hidx"], cx["is_gla"], cx["nv"], cx["j"]
            kh_t, qb_t, v_ts = cx["kh_t"], cx["qb_t"], cx["v_ts"]
            Zb = flat(PSR["Z"]).bitcast(BF16)
            Zb3 = Zb[0:64, :].rearrange("p (c d) -> p c d", c=16)
            Zf = flat(PSR["Z"]).rearrange("p (c d) -> p c d", c=8)
            pt = None
            for c in range(16):
                pt = P.op("pe", lambda e, c=c: e.transpose(Zb3[:, c, :], chunk(Kh, c), ident_b), [kh_t, psfree["Z"], t0] if c == 0 else [])
            kt_t = P.op("act", lambda e: e.activation(Ktok, Zb3, AF.Identity), [pt, qb_t])
            psfree["Z"] = kt_t
            st8 = {"vtok_free": None, "sb_free": None}

            def scan_state(blk):
                pt_ = None
                for c in range(16):
                    pt_ = P.op("pe", lambda e, c=c, blk=blk: e.transpose(Zb3[:, c, :], chunk(vT[:, blk, :], c), ident_b),
                               [v_ts[blk], psfree["Z"]] if c == 0 else [])
                vt_t = P.op("act", lambda e: e.activation(vtok, Zb3, AF.Identity), [pt_, st8["vtok_free"]])
                psfree["Z"] = vt_t
                Spp = [Sst2[:, blk, 0, :], Sst2[:, blk, 1, :]]
                s_t = P.op("dve", lambda e: e.memset(Spp[0], 0.0), [st8["sb_free"]])
                cp_prev = None
                for hh in range(2):
                    pp = None
                    for c8 in range(8):
                        c = hh * 8 + c8
                        pp = P.op("pe", lambda e, c=c, c8=c8: e.matmul(Zf[:, c8, :], Ktok[:, c, :], vtok[:, c, :], start=True, stop=True),
                                  [vt_t, kt_t, psfree["Z"]] if c8 == 0 else [])
                    if hh == 0 and blk == nv - 1 and nxt is not None:
                        cx["pre_gate"] = gate_slab(nxt)
                    for c8 in range(8):
                        c = hh * 8 + c8
                        cp = P.op("act", lambda e, c=c: e.activation(Sb[:, c, :], Spp[c % 2], AF.Identity), [s_t, st8["sb_free"]])
                        s_t = P.op("dve", lambda e, c=c, c8=c8: e.scalar_tensor_tensor(Spp[(c + 1) % 2], Spp[c % 2], dec[:, c:c + 1], Zf[:, c8, :], ALU.mult, ALU.add),
                                   [pp, cp_prev] if c8 == 0 else [cp_prev])
                        cp_prev = cp
                    psfree["Z"] = s_t
                st8["cp_last"] = cp_prev
                return s_t

            def scores():
                pa = None
                for c in range(16):
                    o = flat(PSR["W"])[0:64, c * 64:(c + 1) * 64]
                    pa = P.op("pe", lambda e, o=o, c=c: e.matmul(o, chunk(Kt, c), chunk(Qt, c), start=True, stop=True),
                              [qb_t, psfree["W"]] if c == 0 else [])
                Wv = flat(PSR["W"])[0:64, :].rearrange("p (c d) -> p c d", c=16)
                mb = maskf[:, :].unsqueeze(1).to_broadcast([64, 16, 64])
                a_t = P.op("dve", lambda e: e.tensor_tensor(A_sb.rearrange("p (c d) -> p c d", c=16), Wv, mb, ALU.mult), [pa, t_mk, kt_t])
                psfree["W"] = a_t
                return a_t

            def outputs(blk, s_tok, a_tok):
                po = None
                for c in range(16):
                    o = flat(PSR["W"])[:, c * 64:(c + 1) * 64]
                    P.op("pe", lambda e, o=o, c=c: e.matmul(o, vtok[:, c, :], chunk(A_sb, c), start=True, stop=False),
                         [psfree["W"], s_tok, a_tok, cps[blk]] if c == 0 else [])
                    po = P.op("pe", lambda e, o=o, c=c: e.matmul(o, Sb[:, c, :], chunk(Qt, c), start=False, stop=True))
                st8["vtok_free"] = po
                st8["sb_free"] = po
                o_t = P.op("act", lambda e, blk=blk: e.copy(o_raw[:, blk, :], flat(PSR["W"])), [po, prev_fin])
                psfree["W"] = o_t
                return po

            ncol = nv * 128
            srcd = cc_src[hidx].ap(); dstd = cc_dst[hidx].ap()

            def exchange(s_tok):
                d1 = P.dma("sp", lambda e: e.dma_start(out=srcd.rearrange("p (b d) -> p b d", b=nv), in_=Sst2[:, 0:nv, 0, :]), "cs", deps=[s_tok])
                cc_count[0] += 1
                P.wait("pool", d1)
                P.q["pool"].append(("cc", srcd, dstd))
                cct = ("cc", cc_count[0])
                return P.dma("sp", lambda e: e.dma_start(out=Sin[:, 0:ncol], in_=dstd[0:128, :]), "cl", deps=[cct])

            cps = {}
            if nv == 1:
                s_t = scan_state(0); cps[0] = st8["cp_last"]
                d2 = exchange(s_t)
                a_t = scores()
                po = outputs(0, s_t, a_t)
            else:
                s0 = scan_state(0); cps[0] = st8["cp_last"]
                a_t = scores()
                outputs(0, s0, a_t)
                s_t = scan_state(1); cps[1] = st8["cp_last"]
                d2 = exchange(s_t)
                po = outputs(1, s_t, a_t)
            gc = (C_GO + hidx * 256) if is_gla else (C_HO + j * 128)
            pg = proj(w_in, [(0, 16)], gc, 128, u_rhs, "X")
            g2 = P.op("act", lambda e: e.activation(sg, flat(PSR["X"]), AF.Silu), [pg, prev_fin])
            psfree["X"] = g2
            cx.update({"d2": d2, "g2": g2, "po": po, "s_t": s_t, "ncol": ncol})

        def head_tail(cx):
            hidx, is_gla, nv, j, ncol = cx["hidx"], cx["is_gla"], cx["nv"], cx["j"], cx["ncol"]
            qb_buf = cx["Qb"]
            si_t = P.op("dve", lambda e: e.tensor_scalar(Sin_b[:, 0:ncol], Sin[:, 0:ncol], par[:, P_FLAG:P_FLAG + 1], None, ALU.mult), [cx["d2"], t_par])
            ss_t = None
            for blk in range(nv):
                pcx = None
                for half in range(2):
                    o = flat(PSR["W"])[:, half * 512:(half + 1) * 512]
                    pcx = P.op("pe", lambda e, o=o, blk=blk, half=half: e.matmul(o, Sin_b[:, blk * 128:(blk + 1) * 128], qb_buf[:, half * 512:(half + 1) * 512], start=True, stop=True),
                               [si_t, psfree["W"]] if half == 0 else [])
                ad_t = P.op("dve", lambda e, blk=blk: e.tensor_tensor(o_raw[:, blk, :], o_raw[:, blk, :], flat(PSR["W"]), ALU.add), [pcx])
                psfree["W"] = ad_t
                b = 0
                sq_t = P.op("act", lambda e, blk=blk, b=b: e.activation(sqb[:, b, :], o_raw[:, blk, :], AF.Square), [ad_t, sq_free[b]])
                for half in range(2):
                    o = flat(PSR["Z"])[:, half * 512:(half + 1) * 512]
                    r = sqb[:, b, half * 512:(half + 1) * 512]
                    ss_t = P.op("pe", lambda e, o=o, r=r, st=(blk == 0), sp_=(blk == nv - 1): e.matmul(o, ones_b, r, start=st, stop=sp_),
                                [sq_t] + ([psfree["Z"]] if blk == 0 and half == 0 else []))
                sq_free[b] = ss_t
            r_t = rstd_from("Z", nv * 128, ss_t)
            fin = None
            g2 = cx["g2"]
            for blk in range(nv):
                wn = par[:, P_GNORM + blk:P_GNORM + blk + 1] if is_gla else par[:, P_HNORM:P_HNORM + 1]
                gblk = (hidx * 2 + blk) if is_gla else (8 + j)
                if blk == 1:
                    pg = proj(w_in, [(0, 16)], C_GO + hidx * 256 + 128, 128, u_rhs, "X")
                    g2 = P.op("act", lambda e: e.activation(sg, flat(PSR["X"]), AF.Silu), [pg, fin])
                    psfree["X"] = g2
                P.op("dve", lambda e, blk=blk, wn=wn: e.scalar_tensor_tensor(o_raw[:, blk, :], o_raw[:, blk, :], wn, rstd, ALU.mult, ALU.mult), [r_t, t_par])
                fin = P.op("dve", lambda e, blk=blk, gblk=gblk: e.tensor_tensor(oT_all[:, gblk, :], o_raw[:, blk, :], sg, ALU.mult), [g2])
            return fin

        fin_prev = code_t
        qb_bufs = [Qb, sqb[:, 1, :]]
        cx = head_front(0, [code_t], qb_bufs[0])
        for hidx in range(12):
            nxt = hidx + 1
            head_back(cx, fin_prev, nxt if nxt < 12 else None)
            if nxt < 12:
                cx_next = head_front(nxt, [cx["po"], cx["s_t"]], qb_bufs[nxt % 2], pre_gate=cx.get("pre_gate"))
                fin_prev = head_tail(cx)
                cx = cx_next
            else:
                fin_prev = head_tail(cx)
        hb[0] = fin_prev
        HB = list(hb)

        if _SUB[0] == 50:
            dl = None
            for i in range(NKB):
                dl = P.op("dve", lambda e, i=i: e.tensor_scalar(hT[:, i, :], oT_all[:, i, :], 1.0, None, ALU.mult), HB)
            fin_deps = [dl]
        else:
            mg_last = None
            tA = [tmpA, tmpB]
            tH = [rstd[:, 0:512], rstd[:, 512:1024]]
            for i in range(NKB):
                pzg = proj(w_in, [(0, 16)], C_ZG + i * 128, 128, u_rhs, "X", extra_deps=HB)
                pzh = proj(w_in, [(0, 16)], C_ZH + i * 128, 128, u_rhs, "Y")
                pbg = proj(wbg, [(0, 8)], i * 128, 128, lambda k, half: oT_all[:, k, half * 512:(half + 1) * 512], "W", extra_deps=HB)
                pbh = proj(wbh, [(0, 8)], i * 128, 128, lambda k, half: oT_all[:, 8 + k, half * 512:(half + 1) * 512], "Z")
                for half in range(2):
                    sl = slice(half * 512, (half + 1) * 512)
                    a1 = P.op("act", lambda e, i=i, sl=sl, half=half: e.activation(tA[half][:, :], flat(PSR["X"])[:, sl], AF.Sigmoid, bias=par[:, P_BB0 + i:P_BB0 + i + 1]), [pzg, mg_last, t_par])
                psfree["X"] = a1
                for half in range(2):
                    sl = slice(half * 512, (half + 1) * 512)
                    a2 = P.op("act", lambda e, i=i, sl=sl, half=half: e.activation(tH[half], flat(PSR["Y"])[:, sl], AF.Sigmoid, bias=par[:, P_BB1 + i:P_BB1 + i + 1]), [pzh, mg_last])
                psfree["Y"] = a2
                for half in range(2):
                    sl = slice(half * 512, (half + 1) * 512)
                    d1_ = P.op("dve", lambda e, sl=sl, half=half: e.tensor_tensor(tA[half][:, :], tA[half][:, :], flat(PSR["W"])[:, sl], ALU.mult), [a1, pbg])
                psfree["W"] = d1_
                for half in range(2):
                    sl = slice(half * 512, (half + 1) * 512)
                    d2_ = P.op("dve", lambda e, sl=sl, half=half: e.tensor_tensor(tH[half], tH[half], flat(PSR["Z"])[:, sl], ALU.mult), [a2, pbh])
                psfree["Z"] = d2_
                for half in range(2):
                    sl = slice(half * 512, (half + 1) * 512)
                    mg_last = P.op("dve", lambda e, i=i, sl=sl, half=half: e.tensor_tensor(mergedT[:, i, sl], tA[half][:, :], tH[half], ALU.add))
            h2_t = down_proj(w_out, NKB, mergedT, P_MPOST, 1.0, extra_deps=[mg_last])

            fin_deps = [h2_t]
            if STOP >= 3:
                h3_t = ffn(w2g, w2u, w2d, P_F2PRE, P_F2POST, deps=[h2_t], final=True)
                fin_deps = [h3_t]


    h3_t = fin_deps
    t_id3 = P.dma("sp", lambda e: e.dma_start(out=ident_f, in_=cident[:, :]), "m1", deps=[marks.get("dp_done", h3_t)] if STOP >= 1 and _SUB[0] == 9 else [h3_t])
    halves = marks.get("half") if (STOP >= 3 and _SUB[0] == 9) else None
    os_free = [None, None]
    st_toks = []
    for t in range(8):
        sbuf = ostage[t]
        evs = []
        for g in range(4):
            reg = grp_regions[g]
            pe_t = None
            for q4 in range(4):
                i = g * 4 + q4
                o = flat(PSR[reg])[:, q4 * 128:(q4 + 1) * 128]
                inn = hT[:, i, t * 128:(t + 1) * 128]
                hdep = halves[t // 4] if halves else h3_t
                deps = [hdep, t_id3, psfree[reg]] if q4 == 0 else []
                pe_t = P.op("pe", lambda e, o=o, inn=inn: e.transpose(o, inn, ident_f), deps)
            dst = sbuf[:, g * 512:(g + 1) * 512]
            srcv = flat(PSR[reg])[:, 0:512]
            if g % 2 == 0 or (halves and t < 4):
                ev = P.op("act", lambda e, dst=dst, srcv=srcv: e.copy(dst, srcv), [pe_t])
            else:
                ev = P.op("dve", lambda e, dst=dst, srcv=srcv: e.tensor_copy(dst, srcv), [pe_t])
            psfree[reg] = ev
            evs.append(ev)
        stt = P.dma("sp", lambda e, sbuf=sbuf, t=t: e.dma_start(out=out[t * 128:(t + 1) * 128, :], in_=sbuf), f"os{t}", deps=evs)
        os_free[t % 2] = stt
        st_toks.append(stt)
    P.wait("sp", st_toks)

    sem_names = sorted(set(P.dcnt.keys()))
    sems = {}
    for e in ("pe", "act", "dve"):
        sems[e] = nc.alloc_semaphore(name=f"ms_{e}")
    for s in sem_names:
        sems[s] = nc.alloc_semaphore(name=f"d_{s}")
    sems["cc"] = nc.alloc_semaphore(name="ccsem")

    for e in ("pe", "act", "dve"):
        n = 0
        for item in P.q[e]:
            if item[0] == "op" and item[2].used:
                n += 1
                item[2].ms = n

    def replay(ename, eh):
        seen = {}
        for item in P.q[ename]:
            kind = item[0]
            if kind == "wait":
                tok = item[1]
                if isinstance(tok, Tok):
                    key, val = tok.eng, tok.ms
                else:
                    key, val = tok
                if seen.get(key, 0) >= val:
                    continue
                seen[key] = val
                eh.wait_ge(sems[key], val)
            elif kind == "op":
                ins = item[1](eh)
                if item[2].used:
                    ins.then_inc(sems[ename], 1)
            elif kind == "dma":
                item[1](eh).then_inc(sems[item[2]], 16)
            elif kind == "cc":
                eh.collective_compute("AllGather", ALU.bypass, replica_groups=[[0, 1], [2, 3], [4, 5], [6, 7]],
                                      ins=[item[1].opt()], outs=[item[2].opt()]).then_inc(sems["cc"])

    with nc.Block() as block:
        @block.tensor
        def _(e):
            replay("pe", e)

        @block.scalar
        def _(e):
            replay("act", e)

        @block.vector
        def _(e):
            replay("dve", e)

        @block.gpsimd
        def _(e):
            replay("pool", e)

        @block.sync
        def _(e):
            replay("sp", e)
    nc._declared = declared
    return nc


_NC = [None]
_STOP = [3]
_SUB = [9]


def kernel(**inputs):
    f32 = lambda a: np.ascontiguousarray(np.asarray(a, dtype=np.float32))
    x = f32(inputs["x"])
    fm = lambda v, n: f32(v).reshape(n, 128).T
    par = np.zeros((128, NPAR), np.float32)
    par[:, P_F1PRE:P_F1PRE + 16] = fm(inputs["ffn1_pre_norm"][0], 16)
    par[:, P_F1POST:P_F1POST + 16] = fm(inputs["ffn1_post_norm"][0], 16)
    par[:, P_MPRE:P_MPRE + 16] = fm(inputs["mix_pre_norm"][0], 16)
    par[:, P_MPOST:P_MPOST + 16] = fm(inputs["mix_post_norm"][0], 16)
    par[:, P_F2PRE:P_F2PRE + 16] = fm(inputs["ffn2_pre_norm"][0], 16)
    par[:, P_F2POST:P_F2POST + 16] = fm(inputs["ffn2_post_norm"][0], 16)
    par[:, P_BGK:P_BGK + 4] = fm(inputs["gla_b_gk"][0], 4)
    par[:, P_GNORM:P_GNORM + 2] = fm(inputs["gla_norm"][0], 2)
    par[:, P_HNORM:P_HNORM + 1] = fm(inputs["hgrn_norm"][0], 1)
    par[:, P_L0:P_L0 + 8] = fm(inputs["hgrn_lb_logits"][0], 8)
    par[:, P_L1:P_L1 + 8] = fm(inputs["hgrn_lb_logits"][1], 8)
    par[:, P_BB0:P_BB0 + 16] = fm(inputs["b_branch_gates"][0, 0], 16)
    par[:, P_BB1:P_BB1 + 16] = fm(inputs["b_branch_gates"][0, 1], 16)
    ident = np.eye(128, dtype=np.float32)
    mask = np.triu(np.ones((64, 64), np.float32))
    shared = {
        "c_ident": ident, "c_mask": mask,
        "ffn1_w_gate": f32(inputs["ffn1_w_gate"][0][:, :DFF]), "ffn1_w_up": f32(inputs["ffn1_w_up"][0][:, :DFF]),
        "ffn1_w_down": f32(inputs["ffn1_w_down"][0][:DFF]),
        "ffn2_w_gate": f32(inputs["ffn2_w_gate"][0][:, :DFF]), "ffn2_w_up": f32(inputs["ffn2_w_up"][0][:, :DFF]),
        "ffn2_w_down": f32(inputs["ffn2_w_down"][0][:DFF]),
        "w_in": f32(inputs["w_in"][0]), "gla_w_gk_up": f32(inputs["gla_w_gk_up"][0]),
        "w_branch_gla": f32(inputs["w_branch_gla"][0]), "w_branch_hgrn": f32(inputs["w_branch_hgrn"][0]),
        "w_out": f32(inputs["w_out"][0]),
    }
    if _NC[0] is None:
        _NC[0] = build_program()
    shared = {k: v for k, v in shared.items() if k in _NC[0]._declared}
    in_maps = []
    for c in range(8):
        b, s = c // 2, c % 2
        p = par.copy()
        p[:, P_FLAG] = float(s)
        m = dict(shared)
        m["x"] = np.ascontiguousarray(x[b, s * T:(s + 1) * T, :])
        m["params"] = p
        in_maps.append(m)
    if _NC[0] is None:
        _NC[0] = build_program()
    res = run_bass_kernel_spmd(_NC[0], in_maps, core_ids=list(range(8)))
    outp = np.empty((4, 2048, D), np.float32)
    for c in range(8):
        b, s = c // 2, c % 2
        outp[b, s * T:(s + 1) * T, :] = res.results[c]["out"]
    return outp
```
